# Optimizing a Trainium2 kernel written in Bass

```python
import jax, jax.numpy as jnp
from jax import lax
import numpy as np

D_MODEL = 1024
BATCH = 16
SEQ = 4096
DEPTH = 1

CHUNK = 64
SUB_CHUNK = 16
Q_BLOCK = 128
A_WIDTH = D_MODEL // 2
A_HEAD_DIM = 128
A_HEADS = A_WIDTH // A_HEAD_DIM
B_WIDTH = D_MODEL // 2
B_HEAD_DIM = 64
B_HEADS = B_WIDTH // B_HEAD_DIM
N_BRANCH = 2
D_FF = -(-(8 * D_MODEL) // (3 * 256)) * 256
N_IN = 4 * A_WIDTH + 3 * B_WIDTH + B_HEADS + N_BRANCH * D_MODEL
ALPHA = (2.0 * DEPTH) ** 0.25
BETA = (8.0 * DEPTH) ** -0.25
LN_EPS = 1e-5
RMS_EPS = 1e-6
N_MOD = 6

kernel_name = "hybrid_hgrn2_fox_deepnorm_adaln_block"


def layer_norm(x, w, b):
    xf = x.astype(jnp.float32)
    mu = jnp.mean(xf, axis=-1, keepdims=True)
    var = jnp.mean(jnp.square(xf - mu), axis=-1, keepdims=True)
    return ((xf - mu) * lax.rsqrt(var + LN_EPS) * w + b).astype(x.dtype)


def gated_linear_recurrence(q, k, v, logf):
    bsz, seq, heads, dk = q.shape
    dv = v.shape[-1]
    n_chunks = seq // CHUNK
    ns = CHUNK // SUB_CHUNK

    def to_chunks(t):
        return t.reshape(bsz, n_chunks, CHUNK, heads, t.shape[-1]).transpose(1, 0, 3, 2, 4)

    qc, kc, vc, lc = to_chunks(q), to_chunks(k), to_chunks(v), to_chunks(logf)
    bc = jnp.cumsum(lc, axis=3)
    tri = jnp.tril(jnp.ones((SUB_CHUNK, SUB_CHUNK), dtype=bool))
    later = jnp.tril(jnp.ones((ns, ns), dtype=bool), -1)
    eye = jnp.eye(ns, dtype=jnp.float32)

    def step(state, inp):
        qt, kt, vt, bt = inp
        o_inter = jnp.einsum('bhtk,bhkv->bhtv', qt * jnp.exp(bt), state)
        qs = qt.reshape(bsz, heads, ns, SUB_CHUNK, dk)
        ks = kt.reshape(bsz, heads, ns, SUB_CHUNK, dk)
        bs = bt.reshape(bsz, heads, ns, SUB_CHUNK, dk)
        diff = bs[:, :, :, :, None, :] - bs[:, :, :, None, :, :]
        diff = jnp.where(tri[:, :, None], diff, -jnp.inf)
        a_diag = jnp.sum(qs[:, :, :, :, None, :] * ks[:, :, :, None, :, :] * jnp.exp(diff), axis=-1)
        b_ref = bs[:, :, :, -1, :]
        eq = bs[:, :, :, None, :, :] - b_ref[:, :, None, :, None, :]
        eq = jnp.where(later[:, :, None, None], eq, -jnp.inf)
        qd = qs[:, :, :, None] * jnp.exp(eq)
        kd = ks * jnp.exp(b_ref[:, :, :, None, :] - bs)
        a_off = jnp.einsum('bhijtk,bhjsk->bhitjs', qd, kd)
        a = a_off + eye[:, None, :, None] * a_diag[:, :, :, :, None, :]
        a = a.reshape(bsz, heads, CHUNK, CHUNK)
        o = o_inter + jnp.einsum('bhts,bhsv->bhtv', a, vt)
        b_last = bt[:, :, -1, :]
        new_state = (jnp.exp(b_last)[..., None] * state
                     + jnp.einsum('bhsk,bhsv->bhkv', kt * jnp.exp(b_last[:, :, None, :] - bt), vt))
        return new_state, o

    state0 = jnp.zeros((bsz, heads, dk, dv), jnp.float32)
    _, oc = lax.scan(step, state0, (qc, kc, vc, bc))
    return oc.transpose(1, 0, 3, 2, 4).reshape(bsz, seq, heads, dv)


def hgrn2_mixer(q, f_logit, i_in, g, lb, norm_w):
    bsz, seq, _ = q.shape
    dt = q.dtype

    def split(t):
        return t.reshape(bsz, seq, A_HEADS, A_HEAD_DIM).astype(jnp.float32)

    lbh = lb.astype(jnp.float32).reshape(A_HEADS, A_HEAD_DIM)
    f = lbh + (1.0 - lbh) * jax.nn.sigmoid(split(f_logit))
    o = gated_linear_recurrence(split(q), 1.0 - f, split(i_in), jnp.log(f))
    o = o * lax.rsqrt(jnp.mean(jnp.square(o), axis=-1, keepdims=True) + RMS_EPS)
    o = o * norm_w.astype(jnp.float32).reshape(A_HEADS, A_HEAD_DIM) * jax.nn.sigmoid(split(g))
    return o.reshape(bsz, seq, A_WIDTH).astype(dt)


def forgetting_attention(q, k, v, f_logit, f_bias):
    bsz, seq, _ = q.shape
    dt = q.dtype

    def heads(t):
        return t.reshape(bsz, seq, B_HEADS, B_HEAD_DIM).transpose(0, 2, 1, 3)

    qh, kh, vh = heads(q), heads(k), heads(v)
    logf = jax.nn.log_sigmoid((f_logit + f_bias).astype(jnp.float32))
    cum = jnp.cumsum(logf, axis=1).transpose(0, 2, 1)
    scale = B_HEAD_DIM ** -0.5
    outs = []
    for blk in range(seq // Q_BLOCK):
        lo, hi = blk * Q_BLOCK, (blk + 1) * Q_BLOCK
        s = jnp.einsum('bhqd,bhkd->bhqk', qh[:, :, lo:hi], kh[:, :, :hi]).astype(jnp.float32) * scale
        s = s + cum[:, :, lo:hi, None] - cum[:, :, None, :hi]
        causal = (lo + jnp.arange(Q_BLOCK))[:, None] >= jnp.arange(hi)[None, :]
        p = jax.nn.softmax(jnp.where(causal, s, -jnp.inf), axis=-1)
        outs.append(jnp.einsum('bhqk,bhkd->bhqd', p.astype(dt), vh[:, :, :hi]))
    o = jnp.concatenate(outs, axis=2)
    return o.transpose(0, 2, 1, 3).reshape(bsz, seq, B_WIDTH)


def setup_inputs(seed: int = 0) -> dict:
    key = jax.random.key(seed)
    ks = jax.random.split(key, 20)
    L, D = DEPTH, D_MODEL
    nrm = lambda k, shape, s: jax.random.normal(k, shape, jnp.float32) * s
    col_scale = np.ones((N_IN,), np.float32)
    col_scale[2 * A_WIDTH:3 * A_WIDTH] = BETA
    col_scale[4 * A_WIDTH + 2 * B_WIDTH:4 * A_WIDTH + 3 * B_WIDTH] = BETA
    return {
        "x": nrm(ks[0], (BATCH, SEQ, D), 1.0),
        "c": nrm(ks[1], (BATCH, D), 1.0),
        "w_ada": nrm(ks[2], (L, D, N_MOD * D), 0.1 * D ** -0.5),
        "b_ada": nrm(ks[3], (L, N_MOD * D), 0.01),
        "w_in": nrm(ks[4], (L, D, N_IN), D ** -0.5) * jnp.asarray(col_scale),
        "fox_f_bias": nrm(ks[5], (L, B_HEADS), 0.1) + 2.0,
        "lb_logits": nrm(ks[6], (L + 1, A_WIDTH), 0.1),
        "hgrn_norm_w": 1.0 + nrm(ks[7], (L, A_WIDTH), 0.02),
        "w_branch_a": nrm(ks[8], (L, A_WIDTH, D), BETA * A_WIDTH ** -0.5),
        "w_branch_b": nrm(ks[9], (L, B_WIDTH, D), BETA * B_WIDTH ** -0.5),
        "w_out": nrm(ks[10], (L, D, D), BETA * D ** -0.5),
        "ln1_w": 1.0 + nrm(ks[11], (L, D), 0.02),
        "ln1_b": nrm(ks[12], (L, D), 0.01),
        "w_ffn_gate": nrm(ks[13], (L, D, D_FF), BETA * D ** -0.5),
        "w_ffn_up": nrm(ks[14], (L, D, D_FF), BETA * D ** -0.5),
        "w_ffn_down": nrm(ks[15], (L, D_FF, D), BETA * D_FF ** -0.5),
        "ln2_w": 1.0 + nrm(ks[16], (L, D), 0.02),
        "ln2_b": nrm(ks[17], (L, D), 0.01),
    }


def reference(x, c, w_ada, b_ada, w_in, fox_f_bias, lb_logits, hgrn_norm_w, w_branch_a, w_branch_b,
              w_out, ln1_w, ln1_b, w_ffn_gate, w_ffn_up, w_ffn_down, ln2_w, ln2_b):
    splits = list(np.cumsum([A_WIDTH] * 4 + [B_WIDTH] * 3 + [B_HEADS])[:])
    lower_bounds = jnp.cumsum(jax.nn.softmax(lb_logits.astype(jnp.float32), axis=0), axis=0)
    c_act = jax.nn.silu(c)
    for l in range(DEPTH):
        mod = (c_act @ w_ada[l] + b_ada[l])[:, None, :]
        sh1, sc1, g1, sh2, sc2, g2 = jnp.split(mod, N_MOD, axis=-1)
        h = x * (1.0 + sc1) + sh1
        proj = h @ w_in[l]
        aq, af, ai, ag, bq, bk, bv, bf, gates = jnp.split(proj, splits, axis=-1)
        ya = hgrn2_mixer(aq, af, ai, ag, lower_bounds[l], hgrn_norm_w[l])
        yb = forgetting_attention(bq, bk, bv, bf, fox_f_bias[l])
        gate_a, gate_b = jnp.split(jax.nn.sigmoid(gates), N_BRANCH, axis=-1)
        merged = gate_a * (ya @ w_branch_a[l]) + gate_b * (yb @ w_branch_b[l])
        x = layer_norm(ALPHA * x + (1.0 + g1) * (merged @ w_out[l]), ln1_w[l], ln1_b[l])
        h = x * (1.0 + sc2) + sh2
        ffn = (jax.nn.silu(h @ w_ffn_gate[l]) * (h @ w_ffn_up[l])) @ w_ffn_down[l]
        x = layer_norm(ALPHA * x + (1.0 + g2) * ffn, ln2_w[l], ln2_b[l])
    return x
```

```python
from contextlib import ExitStack

import numpy as np
import concourse.bass as bass
import concourse.mybir as mybir
from concourse.bass_utils import run_bass_kernel_spmd

F32 = mybir.dt.float32
BF16 = mybir.dt.bfloat16
AF = mybir.ActivationFunctionType
ALU = mybir.AluOpType

D = 1024
DFF = 2816
ALPHA = 2.0 ** 0.25
LN_EPS = 1e-5
RMS_EPS = 1e-6
NSLOT = 3
ENGS = ("tensor", "vector", "scalar", "gpsimd", "sync")


class Prog:
    def __init__(self, nc):
        self.nc = nc
        self.q = {e: [] for e in ENGS}
        self.cnt = {}
        self.sems = {}
        self.unit = {}
        self.waited = {}
        self.track = {}
        self._ctx = []
        self.names = []
        for e in ENGS:
            self._mksem(e, 1)

    def _mksem(self, key, unit):
        name = "s_" + (key if isinstance(key, str) else "_".join(map(str, key)))
        cm = self.nc.semaphore(name)
        h = cm.__enter__()
        self._ctx.append(cm)
        self.sems[key] = h
        self.cnt[key] = 0
        self.unit[key] = unit

    def close(self):
        for cm in reversed(self._ctx):
            cm.__exit__(None, None, None)

    def _deps(self, eng, mykey, reads, writes, extra=()):
        deps = {}

        def add(k, n):
            if n > deps.get(k, 0):
                deps[k] = n

        for k, n in extra:
            add(k, n)
        for b in reads:
            t = self.track.get(b)
            if t and t[0]:
                add(*t[0])
        for b in writes:
            t = self.track.get(b)
            if t:
                if t[0]:
                    add(*t[0])
                for k, n in t[1].items():
                    add(k, n)
        waits = []
        for k, n in deps.items():
            if k == mykey and eng == "tensor":
                continue
            if self.waited.get((eng, k), 0) < n:
                self.waited[(eng, k)] = n
                waits.append((k, n * self.unit[k]))
        return waits

    def _commit(self, mykey, n, reads, writes):
        for b in reads:
            t = self.track.setdefault(b, [None, {}])
            t[1][mykey] = n
        for b in writes:
            self.track[b] = [(mykey, n), {}]

    def op(self, eng, fn, reads=(), writes=()):
        writes = list(writes) + [k for k in reads if k.startswith("ps")]
        waits = self._deps(eng, eng, reads, writes)
        self.cnt[eng] += 1
        n = self.cnt[eng]
        sem = self.sems[eng]
        sems = self.sems

        names = self.names if eng == "tensor" else None

        def emit(e):
            for k, v in waits:
                e.wait_ge(sems[k], v)
            ins = fn(e)
            ins.then_inc(sem, 1)
            if names is not None:
                names.append(ins.ins.name)

        self.q[eng].append(emit)
        self._commit(eng, n, reads, writes)

    def dma(self, eng, chan, out, in_, reads=(), writes=(), **kw):
        key = ("dma", chan)
        if key not in self.sems:
            self._mksem(key, 16)
        extra = [(key, self.cnt[key])] if self.cnt[key] else []
        waits = self._deps(eng, key, reads, writes, extra)
        self.cnt[key] += 1
        n = self.cnt[key]
        sem = self.sems[key]
        sems = self.sems

        def emit(e):
            for k, v in waits:
                e.wait_ge(sems[k], v)
            e.dma_start(out=out, in_=in_, **kw).then_inc(sem, 16)

        self.q[eng].append(emit)
        self._commit(key, n, reads, writes)

    def wait_chans(self, eng, chans):
        sems = self.sems
        waits = [(("dma", c), self.cnt[("dma", c)] * 16) for c in chans if ("dma", c) in self.sems]

        def emit(e):
            for k, v in waits:
                e.wait_ge(sems[k], v)

        self.q[eng].append(emit)

    def emit(self):
        q = self.q
        with self.nc.Block() as block:

            @block.sync
            def _(e):
                for f in q["sync"]:
                    f(e)

            @block.tensor
            def _(e):
                for f in q["tensor"]:
                    f(e)

            @block.vector
            def _(e):
                for f in q["vector"]:
                    f(e)

            @block.scalar
            def _(e):
                for f in q["scalar"]:
                    f(e)

            @block.gpsimd
            def _(e):
                for f in q["gpsimd"]:
                    f(e)


def build(NSEQ, SEQ, stop_after=None):
    NT = SEQ // 512
    NKT = SEQ // 128
    KTLEN = max(SEQ, 4096)
    nc = bass.Bass("TRN2", target_bir_lowering=False)

    def din(name, shape, dt=F32):
        return nc.dram_tensor(name, shape, dt, kind="ExternalInput").ap()

    x_d = din("x", [NSEQ, SEQ, D])
    c_d = din("c", [NSEQ, D])
    wada_d = din("w_ada", [D, 6 * D])
    bada_d = din("b_ada", [1, 6 * D])
    win_d = din("w_in", [D, 11 * 512])
    wbf_d = din("w_bf", [D, 8])
    fb_d = din("fbias", [1, 8])
    lbl_d = din("lbl", [8, 128])
    nw_d = din("nw", [4, 128])
    wa_d = din("w_a", [512, D])
    wb_d = din("w_b", [512, D])
    wo_d = din("w_o", [D, D])
    wg_d = din("w_g", [D, DFF])
    wu_d = din("w_u", [D, DFF])
    wd_d = din("w_d", [DFF, D])
    lnv_d = [din(n, [1, D]) for n in ("ln1w", "ln1b", "ln2w", "ln2b")]
    y_d = nc.dram_tensor("y", [NSEQ, SEQ, D], F32, kind="ExternalOutput").ap()

    def dscr(name, shape, dt):
        return nc.dram_tensor(name, shape, dt, kind="Internal").ap()

    win_s = dscr("win_s", [11, 128, 4096], BF16)
    wa_s = dscr("wa_s", [2, 128, 2048], BF16)
    wb_s = dscr("wb_s", [2, 128, 2048], BF16)
    wo_s = dscr("wo_s", [2, 128, 4096], BF16)
    wg_s = dscr("wg_s", [6, 128, 4096], BF16)
    wu_s = dscr("wu_s", [6, 128, 4096], BF16)
    wd_s = dscr("wd_s", [6, 128, 4096], BF16)
    modrow_d = dscr("modrow", [NSEQ, 6 * D], F32)

    P = Prog(nc)
    es = ExitStack()

    def sb(name, shape, dt):
        return es.enter_context(nc.sbuf_tensor(name, shape, dt))

    KT = sb("KT", [128, 4, KTLEN], BF16)
    Vst = sb("Vst", [128, NKT, 512], BF16)
    onesb = sb("onesb", [128, 128], BF16)
    ring = sb("ring", [128, NSLOT, 4096], BF16)
    xt = sb("xt", [128, 4, D], F32)
    hT = sb("hT", [128, 8, 512], BF16)
    G = sb("G", [128, 22, 512], BF16)
    TP = sb("TP", [128, 8, 512], F32)
    bc6 = sb("bc6", [128, 6, D], F32)
    PT = sb("PT", [128, 4, 512], BF16)
    Asb = sb("Asb", [128, 4, 512], BF16)
    identf = sb("identf", [128, 128], F32)
    identb = sb("identb", [128, 128], BF16)
    Uf = sb("Uf", [128, 128], F32)
    Ub4 = sb("Ub4", [128, 4, 128], BF16)
    onesf = sb("onesf", [128, 128], F32)
    scanmask = sb("scanmask", [128, 512], F32)
    S = sb("S", [128, 4, 128], F32)
    Sbf = sb("Sbf", [128, 4, 128], BF16)
    tmpS = sb("tmpS", [128, 4, 128], F32)
    CT = sb("CT", [128, NKT + 1, 2, 8], F32)
    nb = sb("nb", [128, 2, NKT, 8], F32)
    modT = sb("modT", [128, NSEQ, 48], F32)
    lbT = sb("lbT", [128, 8], F32)
    omlT = sb("omlT", [128, 4], F32)
    nwT = sb("nwT", [128, 4], F32)
    ebl = sb("ebl", [128, 4, 4], F32)
    fb4 = sb("fb4", [128, 4, 8], F32)
    wbf = sb("wbf", [128, 8, 8], BF16)
    lg = sb("lg", [128, 32], F32)
    lgn = sb("lgn", [128, 32], F32)
    st = sb("st", [128, 4, 2, 6], F32)
    mv = sb("mv", [128, 4, 2], F32)
    rstd = sb("rstd", [128, 4], F32)
    nmr = sb("nmr", [128, 4], F32)
    rl = sb("rl", [128, 512], F32)
    XY1 = sb("XY1", [128, 2, 512], F32)
    nomlT = sb("nomlT", [128, 4], F32)
    lnT = sb("lnT", [128, 16], F32)
    mod2 = sb("mod2", [128, NSEQ, 16], F32)
    cT = sb("cT", [128, 8, NSEQ], F32)
    rows = sb("rows", [48, 128], F32)
    ps = [es.enter_context(nc.psum_tensor("ps%d" % i, [128, 512], F32)) for i in range(8)]
    ps7b = ps[7].bitcast(BF16)
    stage = KT.bitcast(F32)

    def pk(b, q=None):
        return ["ps%d" % b]

    def MM(out, lhsT, rhs, start, stop, r, w):
        P.op("tensor", lambda e: e.matmul(out, lhsT=lhsT, rhs=rhs, start=start, stop=stop), r, w)

    def TR(out, in_, ident, r, w):
        P.op("tensor", lambda e: e.transpose(out=out, in_=in_, identity=ident), r, w)

    def ACT(out, in_, func, r, w, scale=None, bias=None):
        kw = {}
        if scale is not None:
            kw["scale"] = scale
        if bias is not None:
            kw["bias"] = bias
        P.op("scalar", lambda e: e.activation(out=out, in_=in_, func=func, **kw), r, w)

    def TT(eng, out, in0, in1, op, r, w):
        P.op(eng, lambda e: e.tensor_tensor(out=out, in0=in0, in1=in1, op=op), r, w)

    def TS(eng, out, in0, s1, s2, op0, op1, r, w):
        if s2 is None:
            P.op(eng, lambda e: e.tensor_scalar(out=out, in0=in0, scalar1=s1, scalar2=None, op0=op0), r, w)
        else:
            P.op(eng, lambda e: e.tensor_scalar(out=out, in0=in0, scalar1=s1, scalar2=s2, op0=op0, op1=op1), r, w)

    def STT(out, in0, scalar, in1, op0, op1, r, w):
        P.op("vector", lambda e: e.scalar_tensor_tensor(out=out, in0=in0, scalar=scalar, in1=in1, op0=op0, op1=op1), r, w)

    def CP(eng, out, in_, r, w):
        if eng == "scalar":
            P.op("scalar", lambda e: e.copy(out=out, in_=in_), r, w)
        else:
            P.op(eng, lambda e: e.tensor_copy(out=out, in_=in_), r, w)

    def MSET(eng, ap, val, w):
        P.op(eng, lambda e: e.memset(ap, val), (), w)

    tog = [0]

    def alt():
        tog[0] ^= 1
        return "scalar" if tog[0] else "vector"

    rotc = [0]

    def rot():
        rotc[0] = (rotc[0] + 1) % 8
        return rotc[0]

    def Gs(i):
        return G[:, i, :]

    def Gk(i):
        return "G%d" % i

    def TPs(i):
        return TP[:, i, :]

    def TPk(i):
        return "TP%d" % i

    HTK = ["hT%dk%d" % (s4, kc) for s4 in range(4) for kc in range(8)]

    def htk(s4):
        return ["hT%dk%d" % (s4, kc) for kc in range(8)]

    def v3(ap, k):
        return ap.rearrange("p (k n) -> p k n", k=k)

    pieces = []

    def add_piece(ap, n, key):
        pieces.append((ap, n, key))

    state = {"loaded": 0, "next": 0}

    def get_piece(hold=0):
        i = state["next"]
        state["next"] += 1
        hi = min(len(pieces), i + NSLOT - hold)
        while state["loaded"] < hi:
            j = state["loaded"]
            ap, n, key = pieces[j]
            slot = j % NSLOT
            if n == "half":
                P.dma("sync", "ring%d" % slot, v3(ring[:, slot, :], 8)[:, :, 0:256], v3(ap, 8)[:, :, 0:256], reads=[key], writes=["ring%d" % slot])
            else:
                P.dma("sync", "ring%d" % slot, ring[:, slot, 0:n], ap[:, 0:n], reads=[key], writes=["ring%d" % slot])
            state["loaded"] += 1
        return ring[:, i % NSLOT, :], "ring%d" % (i % NSLOT)

    MSET("gpsimd", identf[:], 0.0, ["identf"])
    P.op("gpsimd", lambda e: e.affine_select(out=identf[:], in_=identf[:], pattern=[[-1, 128]], compare_op=ALU.not_equal,
                                             fill=1.0, base=0, channel_multiplier=1), ["identf"], ["identf"])
    CP("gpsimd", identb[:], identf[:], ["identf"], ["identb"])
    MSET("gpsimd", onesf[:], 1.0, ["onesf"])
    MSET("gpsimd", Uf[:], 1.0, ["Uf"])
    P.op("gpsimd", lambda e: e.affine_select(out=Uf[:], in_=Uf[:], pattern=[[1, 128]], compare_op=ALU.is_ge,
                                             fill=0.0, base=0, channel_multiplier=-1), ["Uf"], ["Uf"])
    for i in range(4):
        CP("gpsimd", Ub4[:, i, :], Uf[:], ["Uf"], ["Ub4"])
    MSET("vector", scanmask[:], 1.0, ["scanmask"])
    MSET("vector", scanmask[:, 0:512:128], 0.0, ["scanmask"])
    MSET("vector", onesb[:], 1.0, ["onesb"])

    cvn = [0]

    def conv(out_ap, in_ap, key):
        P.dma("gpsimd", "cv%d" % (cvn[0] % 8), out_ap, in_ap, writes=[key])
        cvn[0] += 1

    def rows3(ap):
        return ap.rearrange("(k p) n -> p k n", p=128)

    conv(wbf[:], rows3(wbf_d), "wbf")
    for j in range(11):
        conv(v3(win_s[j], 8), rows3(win_d[:, j * 512:(j + 1) * 512]), "win%d" % j)
    for j in range(2):
        conv(v3(wa_s[j], 4), rows3(wa_d[:, j * 512:(j + 1) * 512]), "wa%d" % j)
        conv(v3(wb_s[j], 4), rows3(wb_d[:, j * 512:(j + 1) * 512]), "wb%d" % j)
    for j in range(2):
        conv(v3(wo_s[j], 8), rows3(wo_d[:, j * 512:(j + 1) * 512]), "wo%d" % j)
    for j in range(6):
        w = 512 if j < 5 else 256
        conv(v3(wg_s[j], 8)[:, :, 0:w], rows3(wg_d[:, j * 512:j * 512 + w]), "wg%d" % j)
        conv(v3(wu_s[j], 8)[:, :, 0:w], rows3(wu_d[:, j * 512:j * 512 + w]), "wu%d" % j)
    for nh in range(2):
        for g in range(3):
            nk = 8 if g < 2 else 6
            conv(v3(wd_s[nh * 3 + g], 8)[:, 0:nk, :], rows3(wd_d[g * 1024:g * 1024 + nk * 128, nh * 512:(nh + 1) * 512]),
                 "wd%d" % (nh * 3 + g))

    def tile_pieces():
        for j in (0, 2, 5, 1, 6, 3, 4):
            add_piece(win_s[j], 4096, "win%d" % j)
        for nh in range(2):
            add_piece(wa_s[nh], 2048, "wa%d" % nh)
            add_piece(win_s[7 + nh], 4096, "win%d" % (7 + nh))
            add_piece(wb_s[nh], 2048, "wb%d" % nh)
            add_piece(win_s[9 + nh], 4096, "win%d" % (9 + nh))
        for nh in range(2):
            add_piece(wo_s[nh], 4096, "wo%d" % nh)
        for j in range(6):
            add_piece(wg_s[j], 4096 if j < 5 else "half", "wg%d" % j)
            add_piece(wu_s[j], 4096 if j < 5 else "half", "wu%d" % j)
        for nh in range(2):
            for g in range(3):
                add_piece(wd_s[nh * 3 + g], 4096 if g < 2 else 3072, "wd%d" % (nh * 3 + g))

    for _ in range(NSEQ * NT):
        tile_pieces()

    def load_rows_T(src_ap, nrows, dst_ap, dkey):
        P.dma("sync", "misc", rows[0:nrows, :], src_ap, writes=["rows"])
        b = rot()
        TR(ps[b][:, 0:nrows], rows[0:nrows, :], identf[0:nrows, 0:nrows], ["rows", "identf"], pk(b, 0))
        CP("vector", dst_ap, ps[b][:, 0:nrows], pk(b, 0), [dkey])

    load_rows_T(lbl_d, 8, lbT[:], "lbT")
    TT("vector", lbT[:, 0:4], lbT[:, 0:4], lbT[:, 4:8], ALU.subtract, ["lbT"], ["lbT"])
    ACT(lbT[:, 0:4], lbT[:, 0:4], AF.Sigmoid, ["lbT"], ["lbT"])
    TS("vector", omlT[:], lbT[:, 0:4], -1.0, 1.0, ALU.mult, ALU.add, ["lbT"], ["omlT"])
    TS("vector", nomlT[:], omlT[:], -1.0, None, ALU.mult, None, ["omlT"], ["omlT"])
    load_rows_T(nw_d, 4, nwT[:], "nwT")
    load_rows_T(lnv_d[0].rearrange("o (j p) -> (o j) p", p=128), 8, lnT[:, 0:8], "lnT")
    load_rows_T(lnv_d[1].rearrange("o (j p) -> (o j) p", p=128), 8, lnT[:, 8:16], "lnT")
    for s4 in range(4):
        P.dma("sync", "misc", fb4[:, s4, :], fb_d.partition_broadcast(128), writes=["fb4"])
    for i in range(4):
        P.dma("sync", "misc", bc6[:, 2 + i, :], lnv_d[i].partition_broadcast(128), writes=["bc6_%d" % (2 + i)])

    P.dma("sync", "misc", xt[0:NSEQ, 1, :], c_d, writes=["xt1"])
    for kc in range(8):
        b = rot()
        TR(ps[b][:, 0:NSEQ], xt[0:NSEQ, 1, kc * 128:(kc + 1) * 128], identf[0:NSEQ, 0:NSEQ], ["xt1", "identf"], pk(b, 0))
        ACT(cT[:, kc, :], ps[b][:, 0:NSEQ], AF.Silu, pk(b, 0), ["cT"])
    stvs = [stage[:].rearrange("p a b -> p (a b)")[:, 0:8192].rearrange("p (k n) -> p k n", k=8)]
    if NKT * 512 >= 16384:
        stvs.append(Vst.bitcast(F32)[:].rearrange("p a b -> p (a b)")[:, 0:8192].rearrange("p (k n) -> p k n", k=8))
    mr_keys = []
    for j in range(6):
        stv = stvs[j % len(stvs)]
        skey = "KTstage" if j % len(stvs) == 0 else "Vstage"
        P.dma("sync", "stage%d" % (j % len(stvs)), stv, wada_d[:, j * 1024:(j + 1) * 1024].rearrange("(k p) n -> p k n", p=128), writes=[skey])
        P.dma("sync", "misc", xt[0:1, 0, :], bada_d[0:1, j * 1024:(j + 1) * 1024], writes=["xt0"])
        for half in range(2):
            b = rot()
            for kc in range(8):
                MM(ps[b][0:NSEQ, :], cT[:, kc, :], stv[:, kc, half * 512:(half + 1) * 512], kc == 0, False,
                   ["cT", skey], pk(b))
            MM(ps[b][0:NSEQ, :], onesf[0:1, 0:NSEQ], xt[0:1, 0, half * 512:(half + 1) * 512], False, True,
               ["onesf", "xt0"], pk(b))
            ti = (2 * j + half) % 8
            ACT(TP[0:NSEQ, ti, :], ps[b][0:NSEQ, :], AF.Identity, pk(b), [TPk(ti)],
                bias=(1.0 if j in (1, 2, 4, 5) else 0.0))
            key = "mr%d" % (2 * j + half)
            mr_keys.append(key)
            P.dma("sync", "mr%d" % (ti % 4), modrow_d[:, j * 1024 + half * 512:j * 1024 + (half + 1) * 512],
                  TP[0:NSEQ, ti, :], reads=[TPk(ti)], writes=[key])
    out_chans = set()
    marks = []

    class _Stop(Exception):
        pass

    class Rot:
        def __init__(self, banks):
            self.b = banks
            self.i = 0

        def __call__(self):
            v = self.b[self.i % len(self.b)]
            self.i += 1
            return v

    rot8 = Rot(list(range(8)))
    rot01 = Rot([0, 1])
    ps2b = ps[2].bitcast(BF16)
    xgv = G.bitcast(F32)[:].rearrange("p a b -> p (a b)")[:, 0:4096].rearrange("p (s d) -> p s d", s=4)
    XGK = lambda s4: [Gk(4 * s4 + i) for i in range(4)]
    Ubflat = Ub4[:].rearrange("p a b -> p (a b)")
    XA = Asb.bitcast(F32)
    XYs = [(XA[:, 0:2, :].rearrange("p a b -> p (a b)"), XA[:, 2:4, :].rearrange("p a b -> p (a b)"), ["Asb0", "Asb1"], ["Asb2", "Asb3"]),
           (XY1[:, 0, :], XY1[:, 1, :], ["XY1x"], ["XY1y"])]

    def main_loop():
        for s in range(NSEQ):
            P.dma("sync", "misc", rows[0:48, :], modrow_d[s].rearrange("(j p) -> j p", p=128), reads=mr_keys, writes=["rows"])
            b = rot()
            TR(ps[b][:, 0:48], rows[0:48, :], identf[0:48, 0:48], ["rows", "identf"], pk(b, 0))
            CP("vector", modT[:, s, :], ps[b][:, 0:48], pk(b, 0), ["modT"])
            TT("vector", mod2[:, s, 0:8], lnT[:, 0:8], modT[:, s, 32:40], ALU.mult, ["lnT", "modT"], ["mod2"])
            TT("vector", mod2[:, s, 8:16], lnT[:, 8:16], modT[:, s, 32:40], ALU.mult, ["lnT", "modT"], ["mod2"])
            TT("vector", mod2[:, s, 8:16], mod2[:, s, 8:16], modT[:, s, 24:32], ALU.add, ["mod2", "modT"], ["mod2"])

        first_kt_extra = ["KTstage"]

        def ln_stats(s4, buf, bufk, resid, residk):
            STT(buf, resid, ALPHA, buf, ALU.mult, ALU.add, bufk + residk, bufk)
            for i in range(2):
                P.op("vector", lambda e, i=i: e.bn_stats(out=st[:, s4, i, :], in_=buf[:, i * 512:(i + 1) * 512]), bufk, ["st%d_%d" % (s4, i)])
            P.op("vector", lambda e: e.bn_aggr(out=mv[:, s4, :], in_=st[:, s4, :, :].rearrange("p a b -> p (a b)")),
                 ["st%d_0" % s4, "st%d_1" % s4], ["mv%d" % s4])

        def ln_rstd(s4):
            ACT(rstd[:, s4:s4 + 1], mv[:, s4, 1:2], AF.Sqrt, ["mv%d" % s4], ["rstd%d" % s4], bias=LN_EPS)
            P.op("vector", lambda e: e.reciprocal(out=rstd[:, s4:s4 + 1], in_=rstd[:, s4:s4 + 1]), ["rstd%d" % s4], ["rstd%d" % s4])
            STT(nmr[:, s4:s4 + 1], mv[:, s4, 0:1], -1.0, rstd[:, s4:s4 + 1], ALU.mult, ALU.mult,
                ["mv%d" % s4, "rstd%d" % s4], ["nmr%d" % s4])

        def ln_norm(s4, buf, bufk):
            ACT(buf, buf, AF.Identity, bufk + ["rstd%d" % s4, "nmr%d" % s4], bufk, scale=rstd[:, s4:s4 + 1], bias=nmr[:, s4:s4 + 1])

        def transpose_mod(src, srck, scv, shv, mkey):
            for s4 in range(4):
                for half in range(2):
                    b = (6 if s4 % 2 == 0 else 4) + half
                    for q in range(4):
                        kc = half * 4 + q
                        TR(ps[b][:, q * 128:(q + 1) * 128], src(s4)[:, kc * 128:(kc + 1) * 128], identf[:], srck(s4) + ["identf"], pk(b))
                    for q in range(4):
                        kc = half * 4 + q
                        o = hT[:, kc, s4 * 128:(s4 + 1) * 128]
                        if half == 0:
                            ACT(o, ps[b][:, q * 128:(q + 1) * 128], AF.Identity, pk(b) + [mkey], ["hT%dk%d" % (s4, kc)], scale=scv(kc), bias=shv(kc))
                        else:
                            TS("vector", o, ps[b][:, q * 128:(q + 1) * 128], scv(kc), shv(kc), ALU.mult, ALU.add, pk(b) + [mkey], ["hT%dk%d" % (s4, kc)])

        def proj_fm(pv, pkey, j, rhs_of_kc, rkeys, nk=8, rot=rot8):
            b = rot()
            for kc in range(nk):
                MM(ps[b][:, :], pv[:, kc, j * 128:(j + 1) * 128], rhs_of_kc(kc), kc == 0, kc == nk - 1, [pkey] + rkeys, pk(b))
            return b

        def proj_tm(pv, pkey, lhs_of_kc, lkeys, nk=8, rot=rot8):
            b = rot()
            for kc in range(nk):
                MM(ps[b][:, :], lhs_of_kc(kc), pv[:, kc, :], kc == 0, kc == nk - 1, [pkey] + lkeys, pk(b))
            return b

        def load_x(s_, t_):
            for s4 in range(4):
                P.dma("sync", "x%d" % s4, xt[:, s4, :], x_d[s_, t_ * 512 + s4 * 128:t_ * 512 + (s4 + 1) * 128, :], writes=["xt%d" % s4])

        def load_xg(s_, t_):
            for s4 in range(4):
                P.dma("sync", "xg%d" % s4, xgv[:, s4, :], x_d[s_, t_ * 512 + s4 * 128:t_ * 512 + (s4 + 1) * 128, :], writes=XGK(s4))

        def phase_ab(s_):
            transpose_mod(lambda s4: xgv[:, s4, :], XGK,
                          lambda kc: modT[:, s_, 8 + kc:9 + kc], lambda kc: modT[:, s_, kc:kc + 1], "modT")

        def chk(tag):
            marks.append((tag, P.cnt["tensor"]))
            if stop_after == tag:
                raise _Stop()

        for s in range(NSEQ):
            MSET("gpsimd", S[:].rearrange("p a b -> p (a b)"), 0.0, ["S%d" % h for h in range(4)])
            MSET("gpsimd", Sbf[:].rearrange("p a b -> p (a b)"), 0.0, ["Sbf%d" % h for h in range(4)])
            MSET("vector", CT[:, 0, :, :].rearrange("p a b -> p (a b)"), 0.0, ["CT"])
            P.dma("sync", "misc", bc6[:, 0, :], modrow_d[s:s + 1, 2 * D:3 * D].partition_broadcast(128), reads=mr_keys, writes=["bc6_0"])
            P.dma("sync", "misc", bc6[:, 1, :], modrow_d[s:s + 1, 5 * D:6 * D].partition_broadcast(128), reads=mr_keys, writes=["bc6_1"])

            for t in range(NT):
                tok0 = t * 512
                par = t % 2
                nkt = 4 * t + 4
                chk("A")
                if s == 0 and t == 0:
                    load_xg(s, t)
                    load_x(s, t)
                    phase_ab(s)
                hrhs = lambda kc: hT[:, kc, :]
                chk("B")

                pvr, pkey = get_piece()
                pv = v3(pvr, 8)
                for h in range(4):
                    b = proj_fm(pv, pkey, h, hrhs, HTK)
                    ACT(TPs(h), ps[b][:, :], AF.Sigmoid, pk(b), [TPk(h)])
                pvr, pkey = get_piece()
                pv = v3(pvr, 8)
                for s4 in range(4):
                    b = proj_tm(pv, pkey, lambda kc, s4=s4: hT[:, kc, s4 * 128:(s4 + 1) * 128], htk(s4))
                    CP("vector", Gs(12 + s4), ps[b][:, :], pk(b), [Gk(12 + s4)])
                for h in range(4):
                    Xs, Ys, XK, YK = XYs[h % 2]
                    ACT(Xs, TPs(h), AF.Ln, [TPk(h), "omlT", "lbT"], XK, scale=omlT[:, h:h + 1], bias=lbT[:, h:h + 1])
                    P.op("vector", lambda e, Xs=Xs, Ys=Ys: e.tensor_tensor_scan(out=Ys, data0=scanmask[:], data1=Xs, initial=0.0,
                                                                                op0=ALU.mult, op1=ALU.add), XK + ["scanmask"], YK)
                    ACT(TPs(4 + h), Ys, AF.Exp, YK, [TPk(4 + h)])
                    ACT(Xs, Ys, AF.Exp, YK, XK, scale=-1.0)
                    TS("vector", TPs(h), TPs(h), nomlT[:, h:h + 1], omlT[:, h:h + 1], ALU.mult, ALU.add, [TPk(h), "omlT"], [TPk(h)])
                    TT("vector", Gs(4 + h), TPs(h), Xs, ALU.mult, [TPk(h)] + XK, [Gk(4 + h)])
                    CP("vector", ebl[:, h, :], TP[:, 4 + h, 127:512:128], [TPk(4 + h)], ["ebl%d" % h])
                pvr, pkey = get_piece()
                pv = v3(pvr, 8)
                for hp in range(4):
                    b = proj_fm(pv, pkey, hp, hrhs, HTK)
                    CP("scalar" if hp % 2 else "vector", KT[:, hp, tok0:tok0 + 512], ps[b][:, :], pk(b), ["KT%d_%d" % (t, hp)] + first_kt_extra)
                first_kt_extra = []
                pvr, pkey = get_piece()
                pv = v3(pvr, 8)
                for h in range(4):
                    b = proj_fm(pv, pkey, h, hrhs, HTK)
                    TT("vector", Gs(h), ps[b][:, :], TPs(4 + h), ALU.mult, pk(b) + [TPk(4 + h)], [Gk(h)])
                chk("C")

                hsteps = []
                for h in range(4):
                    def s1(h=h):
                        hh = h % 2
                        for c in range(4):
                            TR(ps2b[:, hh * 512 + c * 128:hh * 512 + (c + 1) * 128], G[:, 4 + h, c * 128:(c + 1) * 128], identb[:],
                               [Gk(4 + h), "identb"], pk(2))
                        CP("vector", Gs(8 + h), ps2b[:, hh * 512:(hh + 1) * 512], pk(2), [Gk(8 + h)])
                    hsteps.append(s1)

                    def s2(h=h):
                        for c in range(4):
                            MM(ps[3][:, c * 128:(c + 1) * 128], G[:, 4 + h, c * 128:(c + 1) * 128], G[:, h, c * 128:(c + 1) * 128], True, True,
                               [Gk(4 + h), Gk(h)], pk(3))
                        TT("vector", Asb[:, h, :], ps[3][:, :], Ubflat, ALU.mult, pk(3) + ["Ub4"], ["Asb%d" % h])
                    hsteps.append(s2)
                for c in range(4):
                    for h in range(4):
                        def s3(c=c, h=h):
                            cs = slice(c * 128, (c + 1) * 128)
                            hs = slice(h * 128, (h + 1) * 128)
                            bO = 4 + h
                            bD = 2 + h % 2
                            MM(ps[bO][:, cs], Sbf[:, h, :], G[:, h, cs], True, False, ["Sbf%d" % h, Gk(h)], pk(bO))
                            MM(ps[bO][:, cs], G[:, 12 + c, hs], Asb[:, h, cs], False, True, [Gk(12 + c), "Asb%d" % h], pk(bO))
                            MM(ps[bD][:, hs], G[:, 8 + h, cs], G[:, 12 + c, hs], True, True, [Gk(8 + h), Gk(12 + c)], pk(bD))
                            TT("vector", tmpS[:, h, :], ps[bD][:, hs], S[:, h, :], ALU.add, pk(bD) + ["S%d" % h], ["tmpS%d" % h])
                            TS("gpsimd", S[:, h, :], tmpS[:, h, :], ebl[:, h, c:c + 1], None, ALU.mult, None, ["tmpS%d" % h, "ebl%d" % h], ["S%d" % h])
                            ACT(Sbf[:, h, :], tmpS[:, h, :], AF.Identity, ["tmpS%d" % h, "ebl%d" % h], ["Sbf%d" % h], scale=ebl[:, h, c:c + 1])
                        hsteps.append(s3)
                nsteps = []
                for h in range(4):
                    def s4_(h=h):
                        n1 = 4 + h
                        bO = 4 + h
                        bR = 2 + h % 2
                        ACT(TPs(n1), ps[bO][:, :], AF.Square, pk(bO), [TPk(n1)])
                        MM(ps[bR][:, :], onesf[:], TPs(n1), True, True, ["onesf", TPk(n1)], pk(bR))
                        ACT(TPs(n1), ps[bR][:, :], AF.Ln, pk(bR), [TPk(n1)], scale=1.0 / 128, bias=RMS_EPS)
                        ACT(TPs(n1), TPs(n1), AF.Exp, [TPk(n1)], [TPk(n1)], scale=-0.5)
                        STT(TPs(n1), ps[bO][:, :], nwT[:, h:h + 1], TPs(n1), ALU.mult, ALU.mult, pk(bO) + ["nwT", TPk(n1)], [TPk(n1)])
                        TT("vector", Gs(16 + h), TPs(n1), TPs(h), ALU.mult, [TPk(n1), TPk(h)], [Gk(16 + h)])
                    nsteps.append(s4_)
                hpos = [0]

                def hgrn_some(n):
                    for _ in range(n):
                        if hpos[0] < len(hsteps):
                            hsteps[hpos[0]]()
                            hpos[0] += 1

                pvr, pkey = get_piece()
                pv = v3(pvr, 8)
                for s4 in range(4):
                    b = proj_tm(pv, pkey, lambda kc, s4=s4: hT[:, kc, s4 * 128:(s4 + 1) * 128], htk(s4), rot=rot01)
                    CP("vector", Vst[:, 4 * t + s4, :], ps[b][:, :], pk(b), ["V%d" % (4 * t + s4)])
                    hgrn_some(2)
                b = rot01()
                for s4 in range(4):
                    for kc in range(8):
                        MM(ps[b][:, s4 * 8:(s4 + 1) * 8], hT[:, kc, s4 * 128:(s4 + 1) * 128], wbf[:, kc, :], kc == 0, kc == 7,
                           htk(s4) + ["wbf"], pk(b))
                TT("vector", lg[:], ps[b][:, 0:32], fb4[:].rearrange("p a b -> p (a b)"), ALU.add, pk(b) + ["fb4"], ["lg"])
                ACT(lg[:], lg[:], AF.Exp, ["lg"], ["lg"], scale=-1.0)
                ACT(lg[:], lg[:], AF.Ln, ["lg"], ["lg"], bias=1.0)
                TS("vector", lgn[:], lg[:], -1.0, None, ALU.mult, None, ["lg"], ["lgn"])
                hgrn_some(2)
                b = rot01()
                for s4 in range(4):
                    MM(ps[b][:, s4 * 16:s4 * 16 + 8], Uf[:], lgn[:, s4 * 8:(s4 + 1) * 8], True, True, ["Uf", "lgn"], pk(b))
                    MM(ps[b][:, s4 * 16 + 8:s4 * 16 + 16], onesf[:], lgn[:, s4 * 8:(s4 + 1) * 8], True, True, ["onesf", "lgn"], pk(b))
                for s4 in range(4):
                    j = 4 * t + s4
                    TT("vector", CT[:, j + 1, :, :], ps[b][:, s4 * 16:(s4 + 1) * 16].rearrange("p (a b) -> p a b", a=2),
                       CT[:, j, 1:2, :].broadcast_to([128, 2, 8]), ALU.add, pk(b) + ["CT"], ["CT"])
                TT("vector", nb[:, par, 0:nkt, :], CT[:, 4 * t + 2, 1:2, :].broadcast_to([128, nkt, 8]), CT[:, 1:nkt + 1, 0, :],
                   ALU.subtract, ["CT"], ["nb%d" % par])
                hgrn_some(2)
                pvr, pkey = get_piece()
                pv = v3(pvr, 8)
                for h in range(4):
                    b = proj_fm(pv, pkey, h, hrhs, HTK, rot=rot01)
                    ACT(TPs(h), ps[b][:, :], AF.Sigmoid, pk(b), [TPk(h)])
                    hgrn_some(3)
                hgrn_some(len(hsteps))
                chk("D")

                pvr, pkey = get_piece()
                pv = v3(pvr, 8)
                for hp in range(4):
                    nsteps[hp]()
                    b = proj_fm(pv, pkey, hp, hrhs, HTK, rot=rot01)
                    MSET("gpsimd", G[64:128, 2 * hp, :], 0.0, [Gk(2 * hp)])
                    MSET("gpsimd", G[0:64, 2 * hp + 1, :], 0.0, [Gk(2 * hp + 1)])
                    ACT(G[0:64, 2 * hp, :], ps[b][0:64, :], AF.Identity, pk(b), [Gk(2 * hp)], scale=0.125)
                    TS("vector", G[64:128, 2 * hp + 1, :], ps[b][64:128, :], 0.125, None, ALU.mult, None, pk(b), [Gk(2 * hp + 1)])
                items = [(h, kt) for h in range(8) for kt in range(nkt)]
                pend = []

                def do_pv(it):
                    h, kt, i4, q0 = it
                    bO = 6 + h % 2
                    bL = 1 + h % 2
                    e0 = h - h % 2
                    MM(ps[bO][:, q0:512], Vst[:, kt, e0 * 64:e0 * 64 + 128], PT[:, i4, q0:512], kt == 0, kt == nkt - 1,
                       ["V%d" % kt, "PT%d" % i4], pk(bO))
                    MM(ps[bL][:, q0:512], onesb[:], PT[:, i4, q0:512], kt == 0, kt == nkt - 1, ["onesb", "PT%d" % i4], pk(bL))
                    if kt == nkt - 1:
                        hp, e = divmod(h, 2)
                        po = e * 64
                        P.op("vector", lambda e_: e_.reciprocal(out=rl[po:po + 64, :], in_=ps[bL][po:po + 64, :]), pk(bL), ["rl"])
                        TT("vector", G[po:po + 64, 8 + hp, :], ps[bO][po:po + 64, :], rl[po:po + 64, :], ALU.mult, pk(bO) + ["rl"], [Gk(8 + hp)])

                for i, (h, kt) in enumerate(items):
                    hp, e = divmod(h, 2)
                    r = kt - 4 * t
                    q0 = max(r, 0) * 128
                    bS = 3 + (i % 3)
                    i4 = i % 4
                    MM(ps[bS][:, q0:512], KT[:, hp, kt * 128:(kt + 1) * 128], G[:, h, q0:512], True, True,
                       ["KT%d_%d" % (kt // 4, hp), Gk(h)], pk(bS))
                    ACT(PT[:, i4, q0:512], ps[bS][:, q0:512], AF.Exp, pk(bS) + ["nb%d" % par], ["PT%d" % i4], bias=nb[:, par, kt, h:h + 1])
                    if r >= 0:
                        TT("gpsimd", PT[:, i4, q0:q0 + 128], PT[:, i4, q0:q0 + 128], Ub4[:, 0, :], ALU.mult, ["PT%d" % i4, "Ub4"], ["PT%d" % i4])
                    pend.append((h, kt, i4, q0))
                    if len(pend) > 2:
                        do_pv(pend.pop(0))
                while pend:
                    do_pv(pend.pop(0))
                YBK = [Gk(8 + hp) for hp in range(4)]
                chk("E")

                for nh in range(2):
                    pa_r, pa_k = get_piece()
                    pa = v3(pa_r[:, 0:2048], 4)
                    b1s = [proj_fm(pa, pa_k, dl, lambda kc: G[:, 16 + kc, :], [Gk(16 + k) for k in range(4)], nk=4) for dl in range(4)]
                    ga_r, ga_k = get_piece()
                    ga = v3(ga_r, 8)
                    for dl in range(4):
                        m1 = 2 * dl
                        b3 = proj_fm(ga, ga_k, dl, hrhs, HTK)
                        ACT(TPs(m1), ps[b3][:, :], AF.Sigmoid, pk(b3), [TPk(m1)])
                        TT("vector", TPs(m1), ps[b1s[dl]][:, :], TPs(m1), ALU.mult, pk(b1s[dl]) + [TPk(m1)], [TPk(m1)])
                    pb_r, pb_k = get_piece()
                    pb_ = v3(pb_r[:, 0:2048], 4)
                    b2s = [proj_fm(pb_, pb_k, dl, lambda kc: G[:, 8 + kc, :], YBK, nk=4) for dl in range(4)]
                    gb_r, gb_k = get_piece()
                    gb = v3(gb_r, 8)
                    for dl in range(4):
                        dc = nh * 4 + dl
                        m1 = 2 * dl
                        m2 = m1 + 1
                        b4 = proj_fm(gb, gb_k, dl, hrhs, HTK)
                        ACT(TPs(m2), ps[b4][:, :], AF.Sigmoid, pk(b4), [TPk(m2)])
                        TT("vector", TPs(m2), ps[b2s[dl]][:, :], TPs(m2), ALU.mult, pk(b2s[dl]) + [TPk(m2)], [TPk(m2)])
                        TT("vector", Gs(dc), TPs(m1), TPs(m2), ALU.add, [TPk(m1), TPk(m2)], [Gk(dc)])
                chk("F")

                MK = [Gk(k) for k in range(8)]
                wo0_r, wo0_k = get_piece()
                wo1_r, wo1_k = get_piece(hold=1)
                wov = [(v3(wo0_r, 8), wo0_k), (v3(wo1_r, 8), wo1_k)]
                TA = lambda s4: TP[:, 2 * s4:2 * s4 + 2, :].rearrange("p a b -> p (a b)")
                TAk = lambda s4: [TPk(2 * s4), TPk(2 * s4 + 1)]
                for s4 in range(4):
                    for nh in range(2):
                        pv, pkey = wov[nh]
                        b = proj_tm(pv, pkey, lambda kc, s4=s4: G[:, kc, s4 * 128:(s4 + 1) * 128], MK)
                        TT("vector", TP[:, 2 * s4 + nh, :], ps[b][:, :], bc6[:, 0, nh * 512:(nh + 1) * 512], ALU.mult, pk(b) + ["bc6_0"],
                           [TPk(2 * s4 + nh)])
                    ln_stats(s4, TA(s4), TAk(s4), xt[:, s4, :], ["xt%d" % s4])
                    ln_rstd(s4)
                    ln_norm(s4, TA(s4), TAk(s4))
                chk("G")
                transpose_mod(TA, TAk, lambda kc: mod2[:, s, kc:kc + 1], lambda kc: mod2[:, s, 8 + kc:9 + kc], "mod2")
                for s4 in range(4):
                    TT("vector", xt[:, s4, :], TA(s4), bc6[:, 2, :], ALU.mult, TAk(s4) + ["bc6_2"], ["xt%d" % s4])
                    TT("gpsimd", xt[:, s4, :], xt[:, s4, :], bc6[:, 3, :], ALU.add, ["xt%d" % s4, "bc6_3"], ["xt%d" % s4])
                chk("H")

                for j in range(6):
                    nfl = 4 if j < 5 else 2
                    g_r, g_k = get_piece()
                    gv = v3(g_r, 8)
                    for fl in range(nfl):
                        f1 = (4 * j + fl) % 8
                        bg = proj_fm(gv, g_k, fl, hrhs, HTK)
                        ACT(TPs(f1), ps[bg][:, :], AF.Silu, pk(bg), [TPk(f1)])
                    u_r, u_k = get_piece()
                    uv = v3(u_r, 8)
                    for fl in range(nfl):
                        fc = 4 * j + fl
                        f1 = fc % 8
                        bu = proj_fm(uv, u_k, fl, hrhs, HTK)
                        TT("vector", Gs(fc), TPs(f1), ps[bu][:, :], ALU.mult, [TPk(f1)] + pk(bu), [Gk(fc)])
                chk("I1")
                for nh in range(2):
                    for g in range(3):
                        pvr, pkey = get_piece()
                        pv = v3(pvr, 8)
                        nk = 8 if g < 2 else 6
                        for s4 in range(4):
                            b = nh * 4 + s4
                            for k in range(nk):
                                fc = 8 * g + k
                                MM(ps[b][:, :], G[:, fc, s4 * 128:(s4 + 1) * 128], pv[:, k, :], fc == 0, fc == 21, [pkey, Gk(fc)], pk(b))
                        if nh == 1 and g == 1:
                            if t + 1 < NT:
                                load_xg(s, t + 1)
                            elif s + 1 < NSEQ:
                                load_xg(s + 1, 0)
                    for s4 in range(4):
                        b = nh * 4 + s4
                        TT("vector", TP[:, 2 * s4 + nh, :], ps[b][:, :], bc6[:, 1, nh * 512:(nh + 1) * 512], ALU.mult, pk(b) + ["bc6_1"],
                           [TPk(2 * s4 + nh)])
                if t + 1 < NT:
                    phase_ab(s)
                elif s + 1 < NSEQ:
                    phase_ab(s + 1)
                for s4 in range(4):
                    ln_stats(s4, TA(s4), TAk(s4), xt[:, s4, :], ["xt%d" % s4])
                if t + 1 < NT:
                    load_x(s, t + 1)
                elif s + 1 < NSEQ:
                    load_x(s + 1, 0)
                for s4 in range(4):
                    ln_rstd(s4)
                for s4 in range(4):
                    ln_norm(s4, TA(s4), TAk(s4))
                for s4 in range(4):
                    TT("vector", TA(s4), TA(s4), bc6[:, 4, :], ALU.mult, TAk(s4) + ["bc6_4"], TAk(s4))
                    TT("gpsimd", TA(s4), TA(s4), bc6[:, 5, :], ALU.add, TAk(s4) + ["bc6_5"], TAk(s4))
                    P.dma("sync", "out%d" % s4, y_d[s, tok0 + s4 * 128:tok0 + (s4 + 1) * 128, :], TA(s4), reads=TAk(s4), writes=["y%d" % s4])
                    out_chans.add("out%d" % s4)

    try:
        main_loop()
    except _Stop:
        pass

    P.wait_chans("sync", sorted(out_chans))
    P.emit()
    P.close()
    es.close()
    nc._marks = marks
    nc._pe_names = P.names
    return nc


def _reorder_w_in(w_in):
    return np.ascontiguousarray(np.concatenate(
        [w_in[:, 512:1024], w_in[:, 0:512], w_in[:, 1024:2048], w_in[:, 2048:3584], w_in[:, 3592:5640]], axis=1))


def make_in_maps(inputs, n_cores, nseq):
    f = lambda a: np.ascontiguousarray(np.asarray(a, dtype=np.float32))
    w_in = f(inputs["w_in"][0])
    shared = {
        "w_ada": f(inputs["w_ada"][0]),
        "b_ada": f(inputs["b_ada"][0]).reshape(1, -1),
        "w_in": _reorder_w_in(w_in),
        "w_bf": np.ascontiguousarray(w_in[:, 3584:3592]),
        "fbias": f(inputs["fox_f_bias"][0]).reshape(1, 8),
        "lbl": f(inputs["lb_logits"]).reshape(8, 128),
        "nw": f(inputs["hgrn_norm_w"][0]).reshape(4, 128),
        "w_a": f(inputs["w_branch_a"][0]),
        "w_b": f(inputs["w_branch_b"][0]),
        "w_o": f(inputs["w_out"][0]),
        "w_g": f(inputs["w_ffn_gate"][0]),
        "w_u": f(inputs["w_ffn_up"][0]),
        "w_d": f(inputs["w_ffn_down"][0]),
        "ln1w": f(inputs["ln1_w"][0]).reshape(1, -1),
        "ln1b": f(inputs["ln1_b"][0]).reshape(1, -1),
        "ln2w": f(inputs["ln2_w"][0]).reshape(1, -1),
        "ln2b": f(inputs["ln2_b"][0]).reshape(1, -1),
    }
    x = np.asarray(inputs["x"], dtype=np.float32)
    c = np.asarray(inputs["c"], dtype=np.float32)
    maps = []
    for i in range(n_cores):
        m = dict(shared)
        m["x"] = np.ascontiguousarray(x[i * nseq:(i + 1) * nseq])
        m["c"] = np.ascontiguousarray(c[i * nseq:(i + 1) * nseq])
        maps.append(m)
    return maps


def kernel(**inputs):
    x = np.asarray(inputs["x"])
    B, SEQ, _ = x.shape
    n_cores = 8
    nseq = B // n_cores
    nc = build(nseq, SEQ)
    in_maps = make_in_maps(inputs, n_cores, nseq)
    res = run_bass_kernel_spmd(nc, in_maps, core_ids=list(range(n_cores)))
    return np.concatenate([r["y"] for r in res.results], axis=0).astype(np.float32)
```

```python
from contextlib import ExitStack

import numpy as np
import concourse.bass as bass
import concourse.mybir as mybir
from concourse.bass_utils import run_bass_kernel_spmd

F32 = mybir.dt.float32
BF16 = mybir.dt.bfloat16
AF = mybir.ActivationFunctionType
ALU = mybir.AluOpType

D = 1024
DFF = 2816
ALPHA = 2.0 ** 0.25
LN_EPS = 1e-5
RMS_EPS = 1e-6
NSLOT = 3
ENGS = ("tensor", "vector", "scalar", "gpsimd", "sync")


class Prog:
    def __init__(self, nc):
        self.nc = nc
        self.q = {e: [] for e in ENGS}
        self.cnt = {}
        self.sems = {}
        self.unit = {}
        self.waited = {}
        self.track = {}
        self._ctx = []
        self.names = []
        for e in ENGS:
            self._mksem(e, 1)

    def _mksem(self, key, unit):
        name = "s_" + (key if isinstance(key, str) else "_".join(map(str, key)))
        cm = self.nc.semaphore(name)
        h = cm.__enter__()
        self._ctx.append(cm)
        self.sems[key] = h
        self.cnt[key] = 0
        self.unit[key] = unit

    def close(self):
        for cm in reversed(self._ctx):
            cm.__exit__(None, None, None)

    def _deps(self, eng, mykey, reads, writes, extra=()):
        deps = {}

        def add(k, n):
            if n > deps.get(k, 0):
                deps[k] = n

        for k, n in extra:
            add(k, n)
        for b in reads:
            t = self.track.get(b)
            if t and t[0]:
                add(*t[0])
        for b in writes:
            t = self.track.get(b)
            if t:
                if t[0]:
                    add(*t[0])
                for k, n in t[1].items():
                    add(k, n)
        waits = []
        for k, n in deps.items():
            if k == mykey and eng == "tensor":
                continue
            if self.waited.get((eng, k), 0) < n:
                self.waited[(eng, k)] = n
                waits.append((k, n * self.unit[k]))
        return waits

    def _commit(self, mykey, n, reads, writes):
        for b in reads:
            t = self.track.setdefault(b, [None, {}])
            t[1][mykey] = n
        for b in writes:
            self.track[b] = [(mykey, n), {}]

    def op(self, eng, fn, reads=(), writes=()):
        writes = list(writes) + [k for k in reads if k.startswith("ps")]
        waits = self._deps(eng, eng, reads, writes)
        self.cnt[eng] += 1
        n = self.cnt[eng]
        sem = self.sems[eng]
        sems = self.sems

        names = self.names if eng == "tensor" else None

        def emit(e):
            for k, v in waits:
                e.wait_ge(sems[k], v)
            ins = fn(e)
            ins.then_inc(sem, 1)
            if names is not None:
                names.append(ins.ins.name)

        self.q[eng].append(emit)
        self._commit(eng, n, reads, writes)

    def dma(self, eng, chan, out, in_, reads=(), writes=(), **kw):
        key = ("dma", chan)
        if key not in self.sems:
            self._mksem(key, 16)
        extra = [(key, self.cnt[key])] if self.cnt[key] else []
        waits = self._deps(eng, key, reads, writes, extra)
        self.cnt[key] += 1
        n = self.cnt[key]
        sem = self.sems[key]
        sems = self.sems

        def emit(e):
            for k, v in waits:
                e.wait_ge(sems[k], v)
            e.dma_start(out=out, in_=in_, **kw).then_inc(sem, 16)

        self.q[eng].append(emit)
        self._commit(key, n, reads, writes)

    def wait_chans(self, eng, chans):
        sems = self.sems
        waits = [(("dma", c), self.cnt[("dma", c)] * 16) for c in chans if ("dma", c) in self.sems]

        def emit(e):
            for k, v in waits:
                e.wait_ge(sems[k], v)

        self.q[eng].append(emit)

    def emit(self):
        q = self.q
        with self.nc.Block() as block:

            @block.sync
            def _(e):
                for f in q["sync"]:
                    f(e)

            @block.tensor
            def _(e):
                for f in q["tensor"]:
                    f(e)

            @block.vector
            def _(e):
                for f in q["vector"]:
                    f(e)

            @block.scalar
            def _(e):
                for f in q["scalar"]:
                    f(e)

            @block.gpsimd
            def _(e):
                for f in q["gpsimd"]:
                    f(e)


def build(NSEQ, SEQ, stop_after=None):
    NT = SEQ // 512
    NKT = SEQ // 128
    KTLEN = max(SEQ, 4096)
    nc = bass.Bass("TRN2", target_bir_lowering=False)

    def din(name, shape, dt=F32):
        return nc.dram_tensor(name, shape, dt, kind="ExternalInput").ap()

    x_d = din("x", [NSEQ, SEQ, D])
    c_d = din("c", [NSEQ, D])
    wada_d = din("w_ada", [D, 6 * D])
    bada_d = din("b_ada", [1, 6 * D])
    win_d = din("w_in", [D, 11 * 512])
    wbf_d = din("w_bf", [D, 8])
    fb_d = din("fbias", [1, 8])
    lbl_d = din("lbl", [8, 128])
    nw_d = din("nw", [4, 128])
    wa_d = din("w_a", [512, D])
    wb_d = din("w_b", [512, D])
    wo_d = din("w_o", [D, D])
    wg_d = din("w_g", [D, DFF])
    wu_d = din("w_u", [D, DFF])
    wd_d = din("w_d", [DFF, D])
    lnv_d = [din(n, [1, D]) for n in ("ln1w", "ln1b", "ln2w", "ln2b")]
    y_d = nc.dram_tensor("y", [NSEQ, SEQ, D], F32, kind="ExternalOutput").ap()

    def dscr(name, shape, dt):
        return nc.dram_tensor(name, shape, dt, kind="Internal").ap()

    win_s = dscr("win_s", [11, 128, 4096], BF16)
    wa_s = dscr("wa_s", [2, 128, 2048], BF16)
    wb_s = dscr("wb_s", [2, 128, 2048], BF16)
    wo_s = dscr("wo_s", [2, 128, 4096], BF16)
    wg_s = dscr("wg_s", [6, 128, 4096], BF16)
    wu_s = dscr("wu_s", [6, 128, 4096], BF16)
    wd_s = dscr("wd_s", [6, 128, 4096], BF16)
    modrow_d = dscr("modrow", [NSEQ, 6 * D], F32)

    P = Prog(nc)
    es = ExitStack()

    def sb(name, shape, dt):
        return es.enter_context(nc.sbuf_tensor(name, shape, dt))

    KT = sb("KT", [128, 4, KTLEN], BF16)
    Vst = sb("Vst", [128, NKT, 512], BF16)
    onesb = sb("onesb", [128, 128], BF16)
    ring = sb("ring", [128, NSLOT, 4096], BF16)
    xt = sb("xt", [128, 4, D], F32)
    hT = sb("hT", [128, 8, 512], BF16)
    G = sb("G", [128, 22, 512], BF16)
    TP = sb("TP", [128, 8, 512], F32)
    bc6 = sb("bc6", [128, 6, D], F32)
    PT = sb("PT", [128, 4, 512], BF16)
    Asb = sb("Asb", [128, 4, 512], BF16)
    identf = sb("identf", [128, 128], F32)
    identb = sb("identb", [128, 128], BF16)
    Uf = sb("Uf", [128, 128], F32)
    Ub4 = sb("Ub4", [128, 4, 128], BF16)
    onesf = sb("onesf", [128, 128], F32)
    scanmask = sb("scanmask", [128, 512], F32)
    S = sb("S", [128, 4, 128], F32)
    Sbf = sb("Sbf", [128, 4, 128], BF16)
    tmpS = sb("tmpS", [128, 4, 128], F32)
    CT = sb("CT", [128, NKT + 1, 2, 8], F32)
    nb = sb("nb", [128, 2, NKT, 8], F32)
    modT = sb("modT", [128, NSEQ, 48], F32)
    lbT = sb("lbT", [128, 8], F32)
    omlT = sb("omlT", [128, 4], F32)
    nwT = sb("nwT", [128, 4], F32)
    ebl = sb("ebl", [128, 4, 4], F32)
    fb4 = sb("fb4", [128, 4, 8], F32)
    wbf = sb("wbf", [128, 8, 8], BF16)
    lg = sb("lg", [128, 32], F32)
    lgn = sb("lgn", [128, 32], F32)
    st = sb("st", [128, 4, 2, 6], F32)
    mv = sb("mv", [128, 4, 2], F32)
    rstd = sb("rstd", [128, 4], F32)
    nmr = sb("nmr", [128, 4], F32)
    rl = sb("rl", [128, 512], F32)
    XY1 = sb("XY1", [128, 2, 512], F32)
    nomlT = sb("nomlT", [128, 4], F32)
    lnT = sb("lnT", [128, 16], F32)
    mod2 = sb("mod2", [128, NSEQ, 16], F32)
    cT = sb("cT", [128, 8, NSEQ], F32)
    rows = sb("rows", [48, 128], F32)
    ps = [es.enter_context(nc.psum_tensor("ps%d" % i, [128, 512], F32)) for i in range(8)]
    ps7b = ps[7].bitcast(BF16)
    stage = KT.bitcast(F32)

    def pk(b, q=None):
        return ["ps%d" % b]

    def MM(out, lhsT, rhs, start, stop, r, w):
        P.op("tensor", lambda e: e.matmul(out, lhsT=lhsT, rhs=rhs, start=start, stop=stop), r, w)

    def TR(out, in_, ident, r, w):
        P.op("tensor", lambda e: e.transpose(out=out, in_=in_, identity=ident), r, w)

    def ACT(out, in_, func, r, w, scale=None, bias=None):
        kw = {}
        if scale is not None:
            kw["scale"] = scale
        if bias is not None:
            kw["bias"] = bias
        P.op("scalar", lambda e: e.activation(out=out, in_=in_, func=func, **kw), r, w)

    def TT(eng, out, in0, in1, op, r, w):
        P.op(eng, lambda e: e.tensor_tensor(out=out, in0=in0, in1=in1, op=op), r, w)

    def TS(eng, out, in0, s1, s2, op0, op1, r, w):
        if s2 is None:
            P.op(eng, lambda e: e.tensor_scalar(out=out, in0=in0, scalar1=s1, scalar2=None, op0=op0), r, w)
        else:
            P.op(eng, lambda e: e.tensor_scalar(out=out, in0=in0, scalar1=s1, scalar2=s2, op0=op0, op1=op1), r, w)

    def STT(out, in0, scalar, in1, op0, op1, r, w):
        P.op("vector", lambda e: e.scalar_tensor_tensor(out=out, in0=in0, scalar=scalar, in1=in1, op0=op0, op1=op1), r, w)

    def CP(eng, out, in_, r, w):
        if eng == "scalar":
            P.op("scalar", lambda e: e.copy(out=out, in_=in_), r, w)
        else:
            P.op(eng, lambda e: e.tensor_copy(out=out, in_=in_), r, w)

    def MSET(eng, ap, val, w):
        P.op(eng, lambda e: e.memset(ap, val), (), w)

    tog = [0]

    def alt():
        tog[0] ^= 1
        return "scalar" if tog[0] else "vector"

    rotc = [0]

    def rot():
        rotc[0] = (rotc[0] + 1) % 8
        return rotc[0]

    def Gs(i):
        return G[:, i, :]

    def Gk(i):
        return "G%d" % i

    def TPs(i):
        return TP[:, i, :]

    def TPk(i):
        return "TP%d" % i

    HTK = ["hT%dk%d" % (s4, kc) for s4 in range(4) for kc in range(8)]

    def htk(s4):
        return ["hT%dk%d" % (s4, kc) for kc in range(8)]

    def v3(ap, k):
        return ap.rearrange("p (k n) -> p k n", k=k)

    pieces = []

    def add_piece(ap, n, key):
        pieces.append((ap, n, key))

    state = {"loaded": 0, "next": 0}

    def get_piece(hold=0):
        i = state["next"]
        state["next"] += 1
        hi = min(len(pieces), i + NSLOT - hold)
        while state["loaded"] < hi:
            j = state["loaded"]
            ap, n, key = pieces[j]
            slot = j % NSLOT
            if n == "half":
                P.dma("sync", "ring%d" % slot, v3(ring[:, slot, :], 8)[:, :, 0:256], v3(ap, 8)[:, :, 0:256], reads=[key], writes=["ring%d" % slot])
            else:
                P.dma("sync", "ring%d" % slot, ring[:, slot, 0:n], ap[:, 0:n], reads=[key], writes=["ring%d" % slot])
            state["loaded"] += 1
        return ring[:, i % NSLOT, :], "ring%d" % (i % NSLOT)

    MSET("gpsimd", identf[:], 0.0, ["identf"])
    P.op("gpsimd", lambda e: e.affine_select(out=identf[:], in_=identf[:], pattern=[[-1, 128]], compare_op=ALU.not_equal,
                                             fill=1.0, base=0, channel_multiplier=1), ["identf"], ["identf"])
    CP("gpsimd", identb[:], identf[:], ["identf"], ["identb"])
    MSET("gpsimd", onesf[:], 1.0, ["onesf"])
    MSET("gpsimd", Uf[:], 1.0, ["Uf"])
    P.op("gpsimd", lambda e: e.affine_select(out=Uf[:], in_=Uf[:], pattern=[[1, 128]], compare_op=ALU.is_ge,
                                             fill=0.0, base=0, channel_multiplier=-1), ["Uf"], ["Uf"])
    for i in range(4):
        CP("gpsimd", Ub4[:, i, :], Uf[:], ["Uf"], ["Ub4"])
    MSET("vector", scanmask[:], 1.0, ["scanmask"])
    MSET("vector", scanmask[:, 0:512:128], 0.0, ["scanmask"])
    MSET("vector", onesb[:], 1.0, ["onesb"])

    cvn = [0]

    def conv(out_ap, in_ap, key):
        P.dma("gpsimd", "cv%d" % (cvn[0] % 8), out_ap, in_ap, writes=[key])
        cvn[0] += 1

    def rows3(ap):
        return ap.rearrange("(k p) n -> p k n", p=128)

    conv(wbf[:], rows3(wbf_d), "wbf")
    for j in range(11):
        conv(v3(win_s[j], 8), rows3(win_d[:, j * 512:(j + 1) * 512]), "win%d" % j)
    for j in range(2):
        conv(v3(wa_s[j], 4), rows3(wa_d[:, j * 512:(j + 1) * 512]), "wa%d" % j)
        conv(v3(wb_s[j], 4), rows3(wb_d[:, j * 512:(j + 1) * 512]), "wb%d" % j)
    for j in range(2):
        conv(v3(wo_s[j], 8), rows3(wo_d[:, j * 512:(j + 1) * 512]), "wo%d" % j)
    for j in range(6):
        w = 512 if j < 5 else 256
        conv(v3(wg_s[j], 8)[:, :, 0:w], rows3(wg_d[:, j * 512:j * 512 + w]), "wg%d" % j)
        conv(v3(wu_s[j], 8)[:, :, 0:w], rows3(wu_d[:, j * 512:j * 512 + w]), "wu%d" % j)
    for nh in range(2):
        for g in range(3):
            nk = 8 if g < 2 else 6
            conv(v3(wd_s[nh * 3 + g], 8)[:, 0:nk, :], rows3(wd_d[g * 1024:g * 1024 + nk * 128, nh * 512:(nh + 1) * 512]),
                 "wd%d" % (nh * 3 + g))

    def tile_pieces():
        for j in (0, 2, 5, 1, 6, 3, 4):
            add_piece(win_s[j], 4096, "win%d" % j)
        for nh in range(2):
            add_piece(wa_s[nh], 2048, "wa%d" % nh)
            add_piece(win_s[7 + nh], 4096, "win%d" % (7 + nh))
            add_piece(wb_s[nh], 2048, "wb%d" % nh)
            add_piece(win_s[9 + nh], 4096, "win%d" % (9 + nh))
        for nh in range(2):
            add_piece(wo_s[nh], 4096, "wo%d" % nh)
        for j in range(6):
            add_piece(wg_s[j], 4096 if j < 5 else "half", "wg%d" % j)
            add_piece(wu_s[j], 4096 if j < 5 else "half", "wu%d" % j)
        for nh in range(2):
            for g in range(3):
                add_piece(wd_s[nh * 3 + g], 4096 if g < 2 else 3072, "wd%d" % (nh * 3 + g))

    for _ in range(NSEQ * NT):
        tile_pieces()

    def load_rows_T(src_ap, nrows, dst_ap, dkey):
        P.dma("sync", "misc", rows[0:nrows, :], src_ap, writes=["rows"])
        b = rot()
        TR(ps[b][:, 0:nrows], rows[0:nrows, :], identf[0:nrows, 0:nrows], ["rows", "identf"], pk(b, 0))
        CP("vector", dst_ap, ps[b][:, 0:nrows], pk(b, 0), [dkey])

    load_rows_T(lbl_d, 8, lbT[:], "lbT")
    TT("vector", lbT[:, 0:4], lbT[:, 0:4], lbT[:, 4:8], ALU.subtract, ["lbT"], ["lbT"])
    ACT(lbT[:, 0:4], lbT[:, 0:4], AF.Sigmoid, ["lbT"], ["lbT"])
    TS("vector", omlT[:], lbT[:, 0:4], -1.0, 1.0, ALU.mult, ALU.add, ["lbT"], ["omlT"])
    TS("vector", nomlT[:], omlT[:], -1.0, None, ALU.mult, None, ["omlT"], ["omlT"])
    load_rows_T(nw_d, 4, nwT[:], "nwT")
    load_rows_T(lnv_d[0].rearrange("o (j p) -> (o j) p", p=128), 8, lnT[:, 0:8], "lnT")
    load_rows_T(lnv_d[1].rearrange("o (j p) -> (o j) p", p=128), 8, lnT[:, 8:16], "lnT")
    for s4 in range(4):
        P.dma("sync", "misc", fb4[:, s4, :], fb_d.partition_broadcast(128), writes=["fb4"])
    for i in range(4):
        P.dma("sync", "misc", bc6[:, 2 + i, :], lnv_d[i].partition_broadcast(128), writes=["bc6_%d" % (2 + i)])

    P.dma("sync", "misc", xt[0:NSEQ, 1, :], c_d, writes=["xt1"])
    for kc in range(8):
        b = rot()
        TR(ps[b][:, 0:NSEQ], xt[0:NSEQ, 1, kc * 128:(kc + 1) * 128], identf[0:NSEQ, 0:NSEQ], ["xt1", "identf"], pk(b, 0))
        ACT(cT[:, kc, :], ps[b][:, 0:NSEQ], AF.Silu, pk(b, 0), ["cT"])
    stvs = [stage[:].rearrange("p a b -> p (a b)")[:, 0:8192].rearrange("p (k n) -> p k n", k=8)]
    if NKT * 512 >= 16384:
        stvs.append(Vst.bitcast(F32)[:].rearrange("p a b -> p (a b)")[:, 0:8192].rearrange("p (k n) -> p k n", k=8))
    mr_keys = []
    for j in range(6):
        stv = stvs[j % len(stvs)]
        skey = "KTstage" if j % len(stvs) == 0 else "Vstage"
        P.dma("sync", "stage%d" % (j % len(stvs)), stv, wada_d[:, j * 1024:(j + 1) * 1024].rearrange("(k p) n -> p k n", p=128), writes=[skey])
        P.dma("sync", "misc", xt[0:1, 0, :], bada_d[0:1, j * 1024:(j + 1) * 1024], writes=["xt0"])
        for half in range(2):
            b = rot()
            for kc in range(8):
                MM(ps[b][0:NSEQ, :], cT[:, kc, :], stv[:, kc, half * 512:(half + 1) * 512], kc == 0, False,
                   ["cT", skey], pk(b))
            MM(ps[b][0:NSEQ, :], onesf[0:1, 0:NSEQ], xt[0:1, 0, half * 512:(half + 1) * 512], False, True,
               ["onesf", "xt0"], pk(b))
            ti = (2 * j + half) % 8
            ACT(TP[0:NSEQ, ti, :], ps[b][0:NSEQ, :], AF.Identity, pk(b), [TPk(ti)],
                bias=(1.0 if j in (1, 2, 4, 5) else 0.0))
            key = "mr%d" % (2 * j + half)
            mr_keys.append(key)
            P.dma("sync", "mr%d" % (ti % 4), modrow_d[:, j * 1024 + half * 512:j * 1024 + (half + 1) * 512],
                  TP[0:NSEQ, ti, :], reads=[TPk(ti)], writes=[key])
    out_chans = set()
    marks = []

    class _Stop(Exception):
        pass

    class Rot:
        def __init__(self, banks):
            self.b = banks
            self.i = 0

        def __call__(self):
            v = self.b[self.i % len(self.b)]
            self.i += 1
            return v

    rot8 = Rot(list(range(8)))
    rot01 = Rot([0, 1])
    ps2b = ps[2].bitcast(BF16)
    xgv = G.bitcast(F32)[:].rearrange("p a b -> p (a b)")[:, 0:4096].rearrange("p (s d) -> p s d", s=4)
    XGK = lambda s4: [Gk(4 * s4 + i) for i in range(4)]
    Ubflat = Ub4[:].rearrange("p a b -> p (a b)")
    XA = Asb.bitcast(F32)
    XYs = [(XA[:, 0:2, :].rearrange("p a b -> p (a b)"), XA[:, 2:4, :].rearrange("p a b -> p (a b)"), ["Asb0", "Asb1"], ["Asb2", "Asb3"]),
           (XY1[:, 0, :], XY1[:, 1, :], ["XY1x"], ["XY1y"])]

    def main_loop():
        for s in range(NSEQ):
            P.dma("sync", "misc", rows[0:48, :], modrow_d[s].rearrange("(j p) -> j p", p=128), reads=mr_keys, writes=["rows"])
            b = rot()
            TR(ps[b][:, 0:48], rows[0:48, :], identf[0:48, 0:48], ["rows", "identf"], pk(b, 0))
            CP("vector", modT[:, s, :], ps[b][:, 0:48], pk(b, 0), ["modT"])
            TT("vector", mod2[:, s, 0:8], lnT[:, 0:8], modT[:, s, 32:40], ALU.mult, ["lnT", "modT"], ["mod2"])
            TT("vector", mod2[:, s, 8:16], lnT[:, 8:16], modT[:, s, 32:40], ALU.mult, ["lnT", "modT"], ["mod2"])
            TT("vector", mod2[:, s, 8:16], mod2[:, s, 8:16], modT[:, s, 24:32], ALU.add, ["mod2", "modT"], ["mod2"])

        first_kt_extra = ["KTstage"]

        def ln_stats(s4, buf, bufk, resid, residk):
            STT(buf, resid, ALPHA, buf, ALU.mult, ALU.add, bufk + residk, bufk)
            for i in range(2):
                P.op("vector", lambda e, i=i: e.bn_stats(out=st[:, s4, i, :], in_=buf[:, i * 512:(i + 1) * 512]), bufk, ["st%d_%d" % (s4, i)])
            P.op("vector", lambda e: e.bn_aggr(out=mv[:, s4, :], in_=st[:, s4, :, :].rearrange("p a b -> p (a b)")),
                 ["st%d_0" % s4, "st%d_1" % s4], ["mv%d" % s4])

        def ln_rstd(s4):
            ACT(rstd[:, s4:s4 + 1], mv[:, s4, 1:2], AF.Sqrt, ["mv%d" % s4], ["rstd%d" % s4], bias=LN_EPS)
            P.op("vector", lambda e: e.reciprocal(out=rstd[:, s4:s4 + 1], in_=rstd[:, s4:s4 + 1]), ["rstd%d" % s4], ["rstd%d" % s4])
            STT(nmr[:, s4:s4 + 1], mv[:, s4, 0:1], -1.0, rstd[:, s4:s4 + 1], ALU.mult, ALU.mult,
                ["mv%d" % s4, "rstd%d" % s4], ["nmr%d" % s4])

        def ln_norm(s4, buf, bufk):
            ACT(buf, buf, AF.Identity, bufk + ["rstd%d" % s4, "nmr%d" % s4], bufk, scale=rstd[:, s4:s4 + 1], bias=nmr[:, s4:s4 + 1])

        def transpose_mod(src, srck, scv, shv, mkey):
            for s4 in range(4):
                for half in range(2):
                    b = (6 if s4 % 2 == 0 else 4) + half
                    for q in range(4):
                        kc = half * 4 + q
                        TR(ps[b][:, q * 128:(q + 1) * 128], src(s4)[:, kc * 128:(kc + 1) * 128], identf[:], srck(s4) + ["identf"], pk(b))
                    for q in range(4):
                        kc = half * 4 + q
                        o = hT[:, kc, s4 * 128:(s4 + 1) * 128]
                        if half == 0:
                            ACT(o, ps[b][:, q * 128:(q + 1) * 128], AF.Identity, pk(b) + [mkey], ["hT%dk%d" % (s4, kc)], scale=scv(kc), bias=shv(kc))
                        else:
                            TS("vector", o, ps[b][:, q * 128:(q + 1) * 128], scv(kc), shv(kc), ALU.mult, ALU.add, pk(b) + [mkey], ["hT%dk%d" % (s4, kc)])

        def proj_fm(pv, pkey, j, rhs_of_kc, rkeys, nk=8, rot=rot8):
            b = rot()
            for kc in range(nk):
                MM(ps[b][:, :], pv[:, kc, j * 128:(j + 1) * 128], rhs_of_kc(kc), kc == 0, kc == nk - 1, [pkey] + rkeys, pk(b))
            return b

        def proj_tm(pv, pkey, lhs_of_kc, lkeys, nk=8, rot=rot8):
            b = rot()
            for kc in range(nk):
                MM(ps[b][:, :], lhs_of_kc(kc), pv[:, kc, :], kc == 0, kc == nk - 1, [pkey] + lkeys, pk(b))
            return b

        def load_x(s_, t_):
            for s4 in range(4):
                P.dma(*(("gpsimd", "xp%d" % s4) if (s_, t_) != (0, 0) else ("sync", "x%d" % s4)), xt[:, s4, :], x_d[s_, t_ * 512 + s4 * 128:t_ * 512 + (s4 + 1) * 128, :], writes=["xt%d" % s4])

        def load_xg(s_, t_):
            for s4 in range(4):
                P.dma("sync", "xg%d" % s4, xgv[:, s4, :], x_d[s_, t_ * 512 + s4 * 128:t_ * 512 + (s4 + 1) * 128, :], writes=XGK(s4))

        def phase_ab(s_):
            transpose_mod(lambda s4: xgv[:, s4, :], XGK,
                          lambda kc: modT[:, s_, 8 + kc:9 + kc], lambda kc: modT[:, s_, kc:kc + 1], "modT")

        def chk(tag):
            marks.append((tag, P.cnt["tensor"]))
            if stop_after == tag:
                raise _Stop()

        for s in range(NSEQ):
            MSET("gpsimd", S[:].rearrange("p a b -> p (a b)"), 0.0, ["S%d" % h for h in range(4)])
            MSET("gpsimd", Sbf[:].rearrange("p a b -> p (a b)"), 0.0, ["Sbf%d" % h for h in range(4)])
            MSET("vector", CT[:, 0, :, :].rearrange("p a b -> p (a b)"), 0.0, ["CT"])
            P.dma("sync", "misc", bc6[:, 0, :], modrow_d[s:s + 1, 2 * D:3 * D].partition_broadcast(128), reads=mr_keys, writes=["bc6_0"])
            P.dma("sync", "misc", bc6[:, 1, :], modrow_d[s:s + 1, 5 * D:6 * D].partition_broadcast(128), reads=mr_keys, writes=["bc6_1"])

            for t in range(NT):
                tok0 = t * 512
                par = t % 2
                nkt = 4 * t + 4
                chk("A")
                if s == 0 and t == 0:
                    load_xg(s, t)
                    load_x(s, t)
                    phase_ab(s)
                hrhs = lambda kc: hT[:, kc, :]
                chk("B")

                pvr, pkey = get_piece()
                pv = v3(pvr, 8)
                for h in range(4):
                    b = proj_fm(pv, pkey, h, hrhs, HTK)
                    ACT(TPs(h), ps[b][:, :], AF.Sigmoid, pk(b), [TPk(h)])
                pvr, pkey = get_piece()
                pv = v3(pvr, 8)
                for s4 in range(4):
                    b = proj_tm(pv, pkey, lambda kc, s4=s4: hT[:, kc, s4 * 128:(s4 + 1) * 128], htk(s4))
                    CP("vector", Gs(12 + s4), ps[b][:, :], pk(b), [Gk(12 + s4)])
                for h in range(4):
                    Xs, Ys, XK, YK = XYs[h % 2]
                    ACT(Xs, TPs(h), AF.Ln, [TPk(h), "omlT", "lbT"], XK, scale=omlT[:, h:h + 1], bias=lbT[:, h:h + 1])
                    P.op("vector", lambda e, Xs=Xs, Ys=Ys: e.tensor_tensor_scan(out=Ys, data0=scanmask[:], data1=Xs, initial=0.0,
                                                                                op0=ALU.mult, op1=ALU.add), XK + ["scanmask"], YK)
                    ACT(TPs(4 + h), Ys, AF.Exp, YK, [TPk(4 + h)])
                    ACT(Xs, Ys, AF.Exp, YK, XK, scale=-1.0)
                    TS("vector", TPs(h), TPs(h), nomlT[:, h:h + 1], omlT[:, h:h + 1], ALU.mult, ALU.add, [TPk(h), "omlT"], [TPk(h)])
                    TT("vector", Gs(4 + h), TPs(h), Xs, ALU.mult, [TPk(h)] + XK, [Gk(4 + h)])
                    CP("vector", ebl[:, h, :], TP[:, 4 + h, 127:512:128], [TPk(4 + h)], ["ebl%d" % h])
                pvr, pkey = get_piece()
                pv = v3(pvr, 8)
                for hp in range(4):
                    b = proj_fm(pv, pkey, hp, hrhs, HTK)
                    CP("scalar" if hp % 2 else "vector", KT[:, hp, tok0:tok0 + 512], ps[b][:, :], pk(b), ["KT%d_%d" % (t, hp)] + first_kt_extra)
                first_kt_extra = []
                pvr, pkey = get_piece()
                pv = v3(pvr, 8)
                for h in range(4):
                    b = proj_fm(pv, pkey, h, hrhs, HTK)
                    TT("vector", Gs(h), ps[b][:, :], TPs(4 + h), ALU.mult, pk(b) + [TPk(4 + h)], [Gk(h)])
                chk("C")

                hsteps = []
                for h in range(4):
                    def s1(h=h):
                        hh = h % 2
                        for c in range(4):
                            TR(ps2b[:, hh * 512 + c * 128:hh * 512 + (c + 1) * 128], G[:, 4 + h, c * 128:(c + 1) * 128], identb[:],
                               [Gk(4 + h), "identb"], pk(2))
                        CP("vector", Gs(8 + h), ps2b[:, hh * 512:(hh + 1) * 512], pk(2), [Gk(8 + h)])
                    hsteps.append(s1)

                    def s2(h=h):
                        for c in range(4):
                            MM(ps[3][:, c * 128:(c + 1) * 128], G[:, 4 + h, c * 128:(c + 1) * 128], G[:, h, c * 128:(c + 1) * 128], True, True,
                               [Gk(4 + h), Gk(h)], pk(3))
                        TT("vector", Asb[:, h, :], ps[3][:, :], Ubflat, ALU.mult, pk(3) + ["Ub4"], ["Asb%d" % h])
                    hsteps.append(s2)
                for c in range(4):
                    for h in range(4):
                        def s3(c=c, h=h):
                            cs = slice(c * 128, (c + 1) * 128)
                            hs = slice(h * 128, (h + 1) * 128)
                            bO = 4 + h
                            bD = 2 + h % 2
                            MM(ps[bO][:, cs], Sbf[:, h, :], G[:, h, cs], True, False, ["Sbf%d" % h, Gk(h)], pk(bO))
                            MM(ps[bO][:, cs], G[:, 12 + c, hs], Asb[:, h, cs], False, True, [Gk(12 + c), "Asb%d" % h], pk(bO))
                            MM(ps[bD][:, hs], G[:, 8 + h, cs], G[:, 12 + c, hs], True, True, [Gk(8 + h), Gk(12 + c)], pk(bD))
                            TT("vector", tmpS[:, h, :], ps[bD][:, hs], S[:, h, :], ALU.add, pk(bD) + ["S%d" % h], ["tmpS%d" % h])
                            TS("gpsimd", S[:, h, :], tmpS[:, h, :], ebl[:, h, c:c + 1], None, ALU.mult, None, ["tmpS%d" % h, "ebl%d" % h], ["S%d" % h])
                            ACT(Sbf[:, h, :], tmpS[:, h, :], AF.Identity, ["tmpS%d" % h, "ebl%d" % h], ["Sbf%d" % h], scale=ebl[:, h, c:c + 1])
                        hsteps.append(s3)
                nsteps = []
                for h in range(4):
                    def s4_(h=h):
                        n1 = 4 + h
                        bO = 4 + h
                        bR = 2 + h % 2
                        ACT(TPs(n1), ps[bO][:, :], AF.Square, pk(bO), [TPk(n1)])
                        MM(ps[bR][:, :], onesf[:], TPs(n1), True, True, ["onesf", TPk(n1)], pk(bR))
                        ACT(TPs(n1), ps[bR][:, :], AF.Ln, pk(bR), [TPk(n1)], scale=1.0 / 128, bias=RMS_EPS)
                        ACT(TPs(n1), TPs(n1), AF.Exp, [TPk(n1)], [TPk(n1)], scale=-0.5)
                        STT(TPs(n1), ps[bO][:, :], nwT[:, h:h + 1], TPs(n1), ALU.mult, ALU.mult, pk(bO) + ["nwT", TPk(n1)], [TPk(n1)])
                        TT("vector", Gs(16 + h), TPs(n1), TPs(h), ALU.mult, [TPk(n1), TPk(h)], [Gk(16 + h)])
                    nsteps.append(s4_)
                hpos = [0]

                def hgrn_some(n):
                    for _ in range(n):
                        if hpos[0] < len(hsteps):
                            hsteps[hpos[0]]()
                            hpos[0] += 1

                pvr, pkey = get_piece()
                pv = v3(pvr, 8)
                for s4 in range(4):
                    b = proj_tm(pv, pkey, lambda kc, s4=s4: hT[:, kc, s4 * 128:(s4 + 1) * 128], htk(s4), rot=rot01)
                    CP("vector", Vst[:, 4 * t + s4, :], ps[b][:, :], pk(b), ["V%d" % (4 * t + s4)])
                    hgrn_some(2)
                b = rot01()
                for s4 in range(4):
                    for kc in range(8):
                        MM(ps[b][:, s4 * 8:(s4 + 1) * 8], hT[:, kc, s4 * 128:(s4 + 1) * 128], wbf[:, kc, :], kc == 0, kc == 7,
                           htk(s4) + ["wbf"], pk(b))
                TT("vector", lg[:], ps[b][:, 0:32], fb4[:].rearrange("p a b -> p (a b)"), ALU.add, pk(b) + ["fb4"], ["lg"])
                ACT(lg[:], lg[:], AF.Exp, ["lg"], ["lg"], scale=-1.0)
                ACT(lg[:], lg[:], AF.Ln, ["lg"], ["lg"], bias=1.0)
                TS("vector", lgn[:], lg[:], -1.0, None, ALU.mult, None, ["lg"], ["lgn"])
                hgrn_some(2)
                b = rot01()
                for s4 in range(4):
                    MM(ps[b][:, s4 * 16:s4 * 16 + 8], Uf[:], lgn[:, s4 * 8:(s4 + 1) * 8], True, True, ["Uf", "lgn"], pk(b))
                    MM(ps[b][:, s4 * 16 + 8:s4 * 16 + 16], onesf[:], lgn[:, s4 * 8:(s4 + 1) * 8], True, True, ["onesf", "lgn"], pk(b))
                for s4 in range(4):
                    j = 4 * t + s4
                    TT("vector", CT[:, j + 1, :, :], ps[b][:, s4 * 16:(s4 + 1) * 16].rearrange("p (a b) -> p a b", a=2),
                       CT[:, j, 1:2, :].broadcast_to([128, 2, 8]), ALU.add, pk(b) + ["CT"], ["CT"])
                TT("vector", nb[:, par, 0:nkt, :], CT[:, 4 * t + 2, 1:2, :].broadcast_to([128, nkt, 8]), CT[:, 1:nkt + 1, 0, :],
                   ALU.subtract, ["CT"], ["nb%d" % par])
                hgrn_some(2)
                pvr, pkey = get_piece()
                pv = v3(pvr, 8)
                for h in range(4):
                    b = proj_fm(pv, pkey, h, hrhs, HTK, rot=rot01)
                    ACT(TPs(h), ps[b][:, :], AF.Sigmoid, pk(b), [TPk(h)])
                    hgrn_some(3)
                hgrn_some(len(hsteps))
                chk("D")

                pvr, pkey = get_piece()
                pv = v3(pvr, 8)
                for hp in range(4):
                    nsteps[hp]()
                    b = proj_fm(pv, pkey, hp, hrhs, HTK, rot=rot01)
                    MSET("gpsimd", G[64:128, 2 * hp, :], 0.0, [Gk(2 * hp)])
                    MSET("gpsimd", G[0:64, 2 * hp + 1, :], 0.0, [Gk(2 * hp + 1)])
                    ACT(G[0:64, 2 * hp, :], ps[b][0:64, :], AF.Identity, pk(b), [Gk(2 * hp)], scale=0.125)
                    TS("vector", G[64:128, 2 * hp + 1, :], ps[b][64:128, :], 0.125, None, ALU.mult, None, pk(b), [Gk(2 * hp + 1)])
                items = [(h, kt) for h in range(8) for kt in range(nkt)]
                pend = []

                def do_pv(it):
                    h, kt, i4, q0 = it
                    bO = 6 + h % 2
                    bL = 1 + h % 2
                    e0 = h - h % 2
                    MM(ps[bO][:, q0:512], Vst[:, kt, e0 * 64:e0 * 64 + 128], PT[:, i4, q0:512], kt == 0, kt == nkt - 1,
                       ["V%d" % kt, "PT%d" % i4], pk(bO))
                    MM(ps[bL][:, q0:512], onesb[:], PT[:, i4, q0:512], kt == 0, kt == nkt - 1, ["onesb", "PT%d" % i4], pk(bL))
                    if kt == nkt - 1:
                        hp, e = divmod(h, 2)
                        po = e * 64
                        P.op("vector", lambda e_: e_.reciprocal(out=rl[po:po + 64, :], in_=ps[bL][po:po + 64, :]), pk(bL), ["rl"])
                        TT("vector", G[po:po + 64, 8 + hp, :], ps[bO][po:po + 64, :], rl[po:po + 64, :], ALU.mult, pk(bO) + ["rl"], [Gk(8 + hp)])

                for i, (h, kt) in enumerate(items):
                    hp, e = divmod(h, 2)
                    r = kt - 4 * t
                    q0 = max(r, 0) * 128
                    bS = 3 + (i % 3)
                    i4 = i % 4
                    MM(ps[bS][:, q0:512], KT[:, hp, kt * 128:(kt + 1) * 128], G[:, h, q0:512], True, True,
                       ["KT%d_%d" % (kt // 4, hp), Gk(h)], pk(bS))
                    ACT(PT[:, i4, q0:512], ps[bS][:, q0:512], AF.Exp, pk(bS) + ["nb%d" % par], ["PT%d" % i4], bias=nb[:, par, kt, h:h + 1])
                    if r >= 0:
                        TT("gpsimd", PT[:, i4, q0:q0 + 128], PT[:, i4, q0:q0 + 128], Ub4[:, 0, :], ALU.mult, ["PT%d" % i4, "Ub4"], ["PT%d" % i4])
                    pend.append((h, kt, i4, q0))
                    if len(pend) > 2:
                        do_pv(pend.pop(0))
                while pend:
                    do_pv(pend.pop(0))
                YBK = [Gk(8 + hp) for hp in range(4)]
                chk("E")

                for nh in range(2):
                    pa_r, pa_k = get_piece()
                    pa = v3(pa_r[:, 0:2048], 4)
                    b1s = [proj_fm(pa, pa_k, dl, lambda kc: G[:, 16 + kc, :], [Gk(16 + k) for k in range(4)], nk=4) for dl in range(4)]
                    ga_r, ga_k = get_piece()
                    ga = v3(ga_r, 8)
                    for dl in range(4):
                        m1 = 2 * dl
                        b3 = proj_fm(ga, ga_k, dl, hrhs, HTK)
                        ACT(TPs(m1), ps[b3][:, :], AF.Sigmoid, pk(b3), [TPk(m1)])
                        TT("vector", TPs(m1), ps[b1s[dl]][:, :], TPs(m1), ALU.mult, pk(b1s[dl]) + [TPk(m1)], [TPk(m1)])
                    pb_r, pb_k = get_piece()
                    pb_ = v3(pb_r[:, 0:2048], 4)
                    b2s = [proj_fm(pb_, pb_k, dl, lambda kc: G[:, 8 + kc, :], YBK, nk=4) for dl in range(4)]
                    gb_r, gb_k = get_piece()
                    gb = v3(gb_r, 8)
                    for dl in range(4):
                        dc = nh * 4 + dl
                        m1 = 2 * dl
                        m2 = m1 + 1
                        b4 = proj_fm(gb, gb_k, dl, hrhs, HTK)
                        ACT(TPs(m2), ps[b4][:, :], AF.Sigmoid, pk(b4), [TPk(m2)])
                        TT("vector", TPs(m2), ps[b2s[dl]][:, :], TPs(m2), ALU.mult, pk(b2s[dl]) + [TPk(m2)], [TPk(m2)])
                        TT("vector", Gs(dc), TPs(m1), TPs(m2), ALU.add, [TPk(m1), TPk(m2)], [Gk(dc)])
                chk("F")

                MK = [Gk(k) for k in range(8)]
                wo0_r, wo0_k = get_piece()
                wo1_r, wo1_k = get_piece(hold=1)
                wov = [(v3(wo0_r, 8), wo0_k), (v3(wo1_r, 8), wo1_k)]
                TA = lambda s4: TP[:, 2 * s4:2 * s4 + 2, :].rearrange("p a b -> p (a b)")
                TAk = lambda s4: [TPk(2 * s4), TPk(2 * s4 + 1)]
                for s4 in range(4):
                    for nh in range(2):
                        pv, pkey = wov[nh]
                        b = proj_tm(pv, pkey, lambda kc, s4=s4: G[:, kc, s4 * 128:(s4 + 1) * 128], MK)
                        TT("vector", TP[:, 2 * s4 + nh, :], ps[b][:, :], bc6[:, 0, nh * 512:(nh + 1) * 512], ALU.mult, pk(b) + ["bc6_0"],
                           [TPk(2 * s4 + nh)])
                    ln_stats(s4, TA(s4), TAk(s4), xt[:, s4, :], ["xt%d" % s4])
                    ln_rstd(s4)
                    ln_norm(s4, TA(s4), TAk(s4))
                chk("G")
                transpose_mod(TA, TAk, lambda kc: mod2[:, s, kc:kc + 1], lambda kc: mod2[:, s, 8 + kc:9 + kc], "mod2")
                for s4 in range(4):
                    TT("vector", xt[:, s4, :], TA(s4), bc6[:, 2, :], ALU.mult, TAk(s4) + ["bc6_2"], ["xt%d" % s4])
                    TT("gpsimd", xt[:, s4, :], xt[:, s4, :], bc6[:, 3, :], ALU.add, ["xt%d" % s4, "bc6_3"], ["xt%d" % s4])
                chk("H")

                for j in range(6):
                    nfl = 4 if j < 5 else 2
                    g_r, g_k = get_piece()
                    gv = v3(g_r, 8)
                    for fl in range(nfl):
                        f1 = (4 * j + fl) % 8
                        bg = proj_fm(gv, g_k, fl, hrhs, HTK)
                        ACT(TPs(f1), ps[bg][:, :], AF.Silu, pk(bg), [TPk(f1)])
                    u_r, u_k = get_piece()
                    uv = v3(u_r, 8)
                    for fl in range(nfl):
                        fc = 4 * j + fl
                        f1 = fc % 8
                        bu = proj_fm(uv, u_k, fl, hrhs, HTK)
                        TT("vector", Gs(fc), TPs(f1), ps[bu][:, :], ALU.mult, [TPk(f1)] + pk(bu), [Gk(fc)])
                chk("I1")
                for nh in range(2):
                    for g in range(3):
                        pvr, pkey = get_piece()
                        pv = v3(pvr, 8)
                        nk = 8 if g < 2 else 6
                        for s4 in range(4):
                            b = nh * 4 + s4
                            for k in range(nk):
                                fc = 8 * g + k
                                MM(ps[b][:, :], G[:, fc, s4 * 128:(s4 + 1) * 128], pv[:, k, :], fc == 0, fc == 21, [pkey, Gk(fc)], pk(b))
                        if nh == 1 and g == 1:
                            if t + 1 < NT:
                                load_xg(s, t + 1)
                            elif s + 1 < NSEQ:
                                load_xg(s + 1, 0)
                    for s4 in range(4):
                        b = nh * 4 + s4
                        TT("vector", TP[:, 2 * s4 + nh, :], ps[b][:, :], bc6[:, 1, nh * 512:(nh + 1) * 512], ALU.mult, pk(b) + ["bc6_1"],
                           [TPk(2 * s4 + nh)])
                if t + 1 < NT:
                    phase_ab(s)
                elif s + 1 < NSEQ:
                    phase_ab(s + 1)
                for s4 in range(4):
                    ln_stats(s4, TA(s4), TAk(s4), xt[:, s4, :], ["xt%d" % s4])
                if t + 1 < NT:
                    load_x(s, t + 1)
                elif s + 1 < NSEQ:
                    load_x(s + 1, 0)
                for s4 in range(4):
                    ln_rstd(s4)
                for s4 in range(4):
                    ln_norm(s4, TA(s4), TAk(s4))
                for s4 in range(4):
                    TT("vector", TA(s4), TA(s4), bc6[:, 4, :], ALU.mult, TAk(s4) + ["bc6_4"], TAk(s4))
                    TT("gpsimd", TA(s4), TA(s4), bc6[:, 5, :], ALU.add, TAk(s4) + ["bc6_5"], TAk(s4))
                    P.dma("gpsimd", "out%d" % s4, y_d[s, tok0 + s4 * 128:tok0 + (s4 + 1) * 128, :], TA(s4), reads=TAk(s4), writes=["y%d" % s4])
                    out_chans.add("out%d" % s4)

    try:
        main_loop()
    except _Stop:
        pass

    P.wait_chans("sync", sorted(out_chans))
    P.emit()
    P.close()
    es.close()
    nc._marks = marks
    nc._pe_names = P.names
    return nc


def _reorder_w_in(w_in):
    return np.ascontiguousarray(np.concatenate(
        [w_in[:, 512:1024], w_in[:, 0:512], w_in[:, 1024:2048], w_in[:, 2048:3584], w_in[:, 3592:5640]], axis=1))


def make_in_maps(inputs, n_cores, nseq):
    f = lambda a: np.ascontiguousarray(np.asarray(a, dtype=np.float32))
    w_in = f(inputs["w_in"][0])
    shared = {
        "w_ada": f(inputs["w_ada"][0]),
        "b_ada": f(inputs["b_ada"][0]).reshape(1, -1),
        "w_in": _reorder_w_in(w_in),
        "w_bf": np.ascontiguousarray(w_in[:, 3584:3592]),
        "fbias": f(inputs["fox_f_bias"][0]).reshape(1, 8),
        "lbl": f(inputs["lb_logits"]).reshape(8, 128),
        "nw": f(inputs["hgrn_norm_w"][0]).reshape(4, 128),
        "w_a": f(inputs["w_branch_a"][0]),
        "w_b": f(inputs["w_branch_b"][0]),
        "w_o": f(inputs["w_out"][0]),
        "w_g": f(inputs["w_ffn_gate"][0]),
        "w_u": f(inputs["w_ffn_up"][0]),
        "w_d": f(inputs["w_ffn_down"][0]),
        "ln1w": f(inputs["ln1_w"][0]).reshape(1, -1),
        "ln1b": f(inputs["ln1_b"][0]).reshape(1, -1),
        "ln2w": f(inputs["ln2_w"][0]).reshape(1, -1),
        "ln2b": f(inputs["ln2_b"][0]).reshape(1, -1),
    }
    x = np.asarray(inputs["x"], dtype=np.float32)
    c = np.asarray(inputs["c"], dtype=np.float32)
    maps = []
    for i in range(n_cores):
        m = dict(shared)
        m["x"] = np.ascontiguousarray(x[i * nseq:(i + 1) * nseq])
        m["c"] = np.ascontiguousarray(c[i * nseq:(i + 1) * nseq])
        maps.append(m)
    return maps


def kernel(**inputs):
    x = np.asarray(inputs["x"])
    B, SEQ, _ = x.shape
    n_cores = 8
    nseq = B // n_cores
    nc = build(nseq, SEQ)
    in_maps = make_in_maps(inputs, n_cores, nseq)
    res = run_bass_kernel_spmd(nc, in_maps, core_ids=list(range(n_cores)))
    return np.concatenate([r["y"] for r in res.results], axis=0).astype(np.float32)
```
